# Optimizing a Trainium2 kernel written in Bass

```python
import jax
import jax.numpy as jnp
from jax import lax
import numpy as np


D_MODEL = 1024
BATCH = 8
SEQ = 4096
DEPTH = 4

GRID_W = 64
CTX_LEN = 256
N_MIXERS = 2
CONV_WIDTH = 31
GLA_HEADS = 4
GLA_DK = D_MODEL // 2
GLA_DV = D_MODEL
HEAD_K = GLA_DK // GLA_HEADS
HEAD_V = GLA_DV // GLA_HEADS
GATE_RANK = 16
GATE_TAU = 16.0
CHUNK = 64
FFN_DIM = 2816
FFN_CONV = 3
EPS = 1e-6

kernel_name = 'hybrid_conformer_gla_convffn_dit'


def rms_norm(x, g):
    x32 = x.astype(jnp.float32)
    y = x32 * lax.rsqrt(jnp.mean(x32 * x32, axis=-1, keepdims=True) + EPS)
    return (y * g.astype(jnp.float32)).astype(x.dtype)


def layer_norm(x, g, b):
    x32 = x.astype(jnp.float32)
    mu = jnp.mean(x32, axis=-1, keepdims=True)
    var = jnp.mean(jnp.square(x32 - mu), axis=-1, keepdims=True)
    y = (x32 - mu) * lax.rsqrt(var + EPS)
    return (y * g.astype(jnp.float32) + b.astype(jnp.float32)).astype(x.dtype)


def adaln(cond, w, b):
    m = jax.nn.silu(cond) @ w + b
    return jnp.split(m, 6, axis=-1)


def modulate(h, shift, scale):
    return h * (1.0 + scale) + shift


def dwconv1d(x, w, b):
    pad = w.shape[0] // 2
    y = lax.conv_general_dilated(x, w[:, None, :], window_strides=(1,), padding=[(pad, pad)],
                                 dimension_numbers=('NWC', 'WIO', 'NWC'), feature_group_count=x.shape[-1])
    return y + b


def dwconv2d(x, w, b):
    y = lax.conv_general_dilated(x, w[:, :, None, :], window_strides=(1, 1), padding='SAME',
                                 dimension_numbers=('NHWC', 'HWIO', 'NHWC'), feature_group_count=x.shape[-1])
    return y + b


def conformer_conv(h, w1, b1, dw, dwb, ln_g, ln_b, w2, b2):
    u = h @ w1 + b1
    a, gt = jnp.split(u, 2, axis=-1)
    u = dwconv1d(a * jax.nn.sigmoid(gt), dw, dwb)
    u = jax.nn.silu(layer_norm(u, ln_g, ln_b))
    return u @ w2 + b2


def conv_ffn(h, wa, wb, dw, dwb, wo, on_grid):
    a = h @ wa
    if on_grid:
        bsz, n, f = a.shape
        rows = n // GRID_W
        a = dwconv2d(a.reshape(bsz, rows, GRID_W, f), dw, dwb).reshape(bsz, n, f)
    else:
        a = dwconv1d(a, dw[FFN_CONV // 2], dwb)
    return (jax.nn.silu(a) * (h @ wb)) @ wo


def split_heads(t, hd):
    return t.reshape(t.shape[0], t.shape[1], -1, hd)


def gla_log_gate(h, wg1, wg2, bg):
    z = ((h @ wg1) @ wg2 + bg).astype(jnp.float32)
    return split_heads(jax.nn.log_sigmoid(z) / GATE_TAU, HEAD_K)


def gla_chunk_scan(q, k, v, g, s0):
    bsz, n, nh, dk = q.shape
    dv = v.shape[-1]
    nc = n // CHUNK
    mask = jnp.tril(jnp.ones((CHUNK, CHUNK), dtype=bool))[:, :, None]

    def to_chunks(t):
        return t.reshape(bsz, nc, CHUNK, nh, t.shape[-1]).transpose(1, 0, 3, 2, 4)

    def step(s, inp):
        qc, kc, vc, gc = inp
        b = jnp.cumsum(gc, axis=2)
        o_inter = jnp.einsum('bhtd,bhde->bhte', qc * jnp.exp(b), s)
        diff = b[:, :, :, None, :] - b[:, :, None, :, :]
        decay = jnp.exp(jnp.where(mask, diff, -jnp.inf))
        scores = jnp.einsum('bhtd,bhsd,bhtsd->bhts', qc, kc, decay)
        o_intra = jnp.einsum('bhts,bhse->bhte', scores, vc)
        b_end = b[:, :, -1:, :]
        s_new = jnp.exp(b_end[:, :, 0, :])[..., None] * s + jnp.einsum('bhsd,bhse->bhde', kc * jnp.exp(b_end - b), vc)
        return s_new, o_inter + o_intra

    s_fin, o = lax.scan(step, s0, (to_chunks(q), to_chunks(k), to_chunks(v), to_chunks(g)))
    o = o.transpose(1, 0, 3, 2, 4).reshape(bsz, n, nh, dv)
    return o, s_fin


def gla_bidir(q, k, v, g_f, g_b, s_f, s_b):
    o_f, fin_f = gla_chunk_scan(q, k, v, g_f, s_f)
    o_b, fin_b = gla_chunk_scan(q[:, ::-1], k[:, ::-1], v[:, ::-1], g_b[:, ::-1], s_b)
    return o_f + o_b[:, ::-1], fin_f, fin_b


def gla_final_state(k, v, g):
    b = jnp.cumsum(g, axis=1)
    return jnp.einsum('bnhd,bnhe->bhde', k * jnp.exp(b[:, -1:] - b), v)


def gla_mixer(hx, hc, wq, wk, wv, wr, wg1, wg2, bg, norm_g, wo, ctx_out):
    def proj_q(h):
        return split_heads((h @ wq).astype(jnp.float32), HEAD_K) * (HEAD_K ** -0.5)

    def proj_kvg(h):
        k = split_heads((h @ wk).astype(jnp.float32), HEAD_K)
        v = split_heads((h @ wv).astype(jnp.float32), HEAD_V)
        return k, v, gla_log_gate(h, wg1[0], wg2[0], bg[0]), gla_log_gate(h, wg1[1], wg2[1], bg[1])

    def readout(o, h):
        o = rms_norm(o, norm_g).reshape(h.shape[0], h.shape[1], GLA_DV).astype(h.dtype)
        return (o * jax.nn.silu(h @ wr)) @ wo

    kc, vc, gcf, gcb = proj_kvg(hc)
    if ctx_out:
        zeros = jnp.zeros((hc.shape[0], GLA_HEADS, HEAD_K, HEAD_V), jnp.float32)
        oc, s_f, s_b = gla_bidir(proj_q(hc), kc, vc, gcf, gcb, zeros, zeros)
        yc = readout(oc, hc)
    else:
        s_f = gla_final_state(kc, vc, gcf)
        s_b = gla_final_state(kc[:, ::-1], vc[:, ::-1], gcb[:, ::-1])
        yc = None
    kx, vx, gxf, gxb = proj_kvg(hx)
    ox, _, _ = gla_bidir(proj_q(hx), kx, vx, gxf, gxb, s_f, s_b)
    return readout(ox, hx), yc


def setup_inputs(seed: int = 0) -> dict:
    key = jax.random.key(seed)
    ks = iter(jax.random.split(key, 40))
    n_a = (DEPTH + 1) // 2
    n_b = DEPTH // 2
    D = D_MODEL

    def nrm(shape, scale):
        return jax.random.normal(next(ks), shape, jnp.float32) * scale

    def gain(shape):
        return 1.0 + nrm(shape, 0.05)

    return {
        'x': nrm((BATCH, SEQ, D), 1.0),
        'c': nrm((BATCH, D), 1.0),
        'ctx': nrm((BATCH, CTX_LEN, D), 1.0),
        'c_ctx': nrm((D,), 1.0),
        'ada_w': nrm((DEPTH, D, 6 * D), 0.5 * D ** -0.5),
        'ada_b': nrm((DEPTH, 6 * D), 0.02),
        'norm_pre_mix': gain((DEPTH, D)),
        'norm_post_mix': gain((DEPTH, D)),
        'norm_pre_ffn': gain((DEPTH, D)),
        'norm_post_ffn': gain((DEPTH, D)),
        'cf_w1': nrm((n_a, D, 2 * D), D ** -0.5),
        'cf_b1': nrm((n_a, 2 * D), 0.02),
        'cf_dw': nrm((n_a, CONV_WIDTH, D), CONV_WIDTH ** -0.5),
        'cf_dwb': nrm((n_a, D), 0.02),
        'cf_ln_g': gain((n_a, D)),
        'cf_ln_b': nrm((n_a, D), 0.02),
        'cf_w2': nrm((n_a, D, D), D ** -0.5),
        'cf_b2': nrm((n_a, D), 0.02),
        'gla_wq': nrm((n_b, D, GLA_DK), D ** -0.5),
        'gla_wk': nrm((n_b, D, GLA_DK), D ** -0.5),
        'gla_wv': nrm((n_b, D, GLA_DV), D ** -0.5),
        'gla_wr': nrm((n_b, D, GLA_DV), D ** -0.5),
        'gla_wg1': nrm((n_b, 2, D, GATE_RANK), D ** -0.5),
        'gla_wg2': nrm((n_b, 2, GATE_RANK, GLA_DK), GATE_RANK ** -0.5),
        'gla_bg': nrm((n_b, 2, GLA_DK), 0.5),
        'gla_norm_g': gain((n_b, HEAD_V)),
        'gla_wo': nrm((n_b, GLA_DV, D), GLA_DV ** -0.5),
        'ffn_wa': nrm((DEPTH, D, FFN_DIM), D ** -0.5),
        'ffn_wb': nrm((DEPTH, D, FFN_DIM), D ** -0.5),
        'ffn_dw': nrm((DEPTH, FFN_CONV, FFN_CONV, FFN_DIM), 1.0 / FFN_CONV),
        'ffn_dwb': nrm((DEPTH, FFN_DIM), 0.02),
        'ffn_wo': nrm((DEPTH, FFN_DIM, D), FFN_DIM ** -0.5),
    }


def reference(x, c, ctx, c_ctx, ada_w, ada_b, norm_pre_mix, norm_post_mix, norm_pre_ffn, norm_post_ffn,
              cf_w1, cf_b1, cf_dw, cf_dwb, cf_ln_g, cf_ln_b, cf_w2, cf_b2,
              gla_wq, gla_wk, gla_wv, gla_wr, gla_wg1, gla_wg2, gla_bg, gla_norm_g, gla_wo,
              ffn_wa, ffn_wb, ffn_dw, ffn_dwb, ffn_wo):
    for i in range(DEPTH):
        last = i == DEPTH - 1
        j = i // N_MIXERS
        sh1, sc1, g1, sh2, sc2, g2 = [t[:, None, :] for t in adaln(c, ada_w[i], ada_b[i])]
        csh1, csc1, cg1, csh2, csc2, cg2 = adaln(c_ctx, ada_w[i], ada_b[i])
        hx = modulate(rms_norm(x, norm_pre_mix[i]), sh1, sc1)
        if i % N_MIXERS == 0:
            cf = (cf_w1[j], cf_b1[j], cf_dw[j], cf_dwb[j], cf_ln_g[j], cf_ln_b[j], cf_w2[j], cf_b2[j])
            yx = conformer_conv(hx, *cf)
            yc = None if last else conformer_conv(modulate(rms_norm(ctx, norm_pre_mix[i]), csh1, csc1), *cf)
        else:
            hc = modulate(rms_norm(ctx, norm_pre_mix[i]), csh1, csc1)
            yx, yc = gla_mixer(hx, hc, gla_wq[j], gla_wk[j], gla_wv[j], gla_wr[j], gla_wg1[j], gla_wg2[j],
                               gla_bg[j], gla_norm_g[j], gla_wo[j], ctx_out=not last)
        x = x + g1 * rms_norm(yx, norm_post_mix[i])
        ffn = (ffn_wa[i], ffn_wb[i], ffn_dw[i], ffn_dwb[i], ffn_wo[i])
        hx = modulate(rms_norm(x, norm_pre_ffn[i]), sh2, sc2)
        x = x + g2 * rms_norm(conv_ffn(hx, *ffn, on_grid=True), norm_post_ffn[i])
        if not last:
            ctx = ctx + cg1 * rms_norm(yc, norm_post_mix[i])
            hc = modulate(rms_norm(ctx, norm_pre_ffn[i]), csh2, csc2)
            ctx = ctx + cg2 * rms_norm(conv_ffn(hc, *ffn, on_grid=False), norm_post_ffn[i])
    return x
```

```python
import contextlib
import numpy as np
import concourse.bass as bass
import concourse.mybir as mybir
from concourse.bass_utils import run_bass_kernel_spmd

F32 = mybir.dt.float32
BF16 = mybir.dt.bfloat16
I32 = mybir.dt.int32
ALU = mybir.AluOpType
AF = mybir.ActivationFunctionType

PE, ACT, DVE, POOL, SP = "pe", "act", "dve", "pool", "sp"
COMPUTE = (PE, ACT, DVE, POOL)

D = 1024
NSEQ = 4096
NCTX = 256
DEPTH = 4
FF = 2816
NFC = FF // 128
KW = 31
EPS = 1e-6
NCORES = 8


class Buf:
    __slots__ = ("name", "t", "lastw", "readers")

    def __init__(self, name, t=None):
        self.name = name
        self.t = t
        self.lastw = None
        self.readers = []

    def __getitem__(self, idx):
        return self.t[idx]


class Instr:
    __slots__ = ("eng", "fn", "reads", "writes", "deps", "signal", "is_dma", "sem", "val")

    def __init__(self, eng, fn, reads, writes, is_dma):
        self.eng = eng
        self.fn = fn
        self.reads = reads
        self.writes = writes
        self.deps = []
        self.signal = False
        self.is_dma = is_dma
        self.sem = None
        self.val = None


class Prog:
    def __init__(self, nc, st, n_dma_sems=32):
        self.nc = nc
        self.instrs = []
        self.bufs = []
        self.n_dma_sems = n_dma_sems
        self.eng = {PE: nc.tensor, ACT: nc.scalar, DVE: nc.vector, POOL: nc.gpsimd, SP: nc.sync}
        self.esem = {e: st.enter_context(nc.semaphore("s_" + e)) for e in COMPUTE}
        self.dsems = [st.enter_context(nc.semaphore("d%d" % i)) for i in range(n_dma_sems)]
        self.ecount = {e: 0 for e in COMPUTE}
        self.dcount = [0] * n_dma_sems
        self.ndma = 0
        self.waited = {}
        self.total = 0

    def buf(self, name, t=None):
        b = Buf(name, t)
        self.bufs.append(b)
        return b

    def op(self, eng, fn, reads=(), writes=()):
        self._add(Instr(eng, fn, tuple(reads), tuple(writes), False))

    def dma(self, q, out_ap, in_ap, reads=(), writes=(), **kw):
        eng = self.eng[q]
        fn = (lambda eng=eng, o=out_ap, i=in_ap, kw=kw: eng.dma_start(out=o, in_=i, **kw))
        self._add(Instr(q, fn, tuple(reads), tuple(writes), True))

    def _add(self, ins):
        idx = len(self.instrs)
        deps = set()
        for b in ins.reads:
            if b.lastw is not None:
                deps.add(b.lastw)
        for b in ins.writes:
            if b.lastw is not None:
                deps.add(b.lastw)
            deps.update(b.readers)
        deps.discard(idx)
        keep = []
        for d in deps:
            p = self.instrs[d]
            if p.is_dma or ins.is_dma:
                keep.append(d)
                continue
            if p.eng == ins.eng:
                if ins.eng == PE:
                    continue
                raw = any((b.lastw == d) for b in ins.reads)
                if not raw:
                    continue
            keep.append(d)
        ins.deps = keep
        for b in ins.reads:
            b.readers.append(idx)
        for b in ins.writes:
            b.lastw = idx
            b.readers = []
        self.instrs.append(ins)

    def flush(self, final=False):
        instrs = self.instrs
        for ins in instrs:
            for d in ins.deps:
                instrs[d].signal = True
        last = {}
        for i, ins in enumerate(instrs):
            if not ins.is_dma:
                last[ins.eng] = i
        for e, i in last.items():
            instrs[i].signal = True
        nds = self.n_dma_sems
        for ins in instrs:
            e = self.eng[ins.eng]
            need = {}
            for d in ins.deps:
                p = instrs[d]
                key = id(p.sem)
                if key not in need or need[key][1] < p.val:
                    need[key] = (p.sem, p.val)
            if ins.is_dma:
                k = self.ndma % nds
                self.ndma += 1
                if self.dcount[k] > 0:
                    key = id(self.dsems[k])
                    v = self.dcount[k] * 16
                    if key not in need or need[key][1] < v:
                        need[key] = (self.dsems[k], v)
            for key, (sem, v) in need.items():
                wk = (ins.eng, key)
                if self.waited.get(wk, -1) >= v:
                    continue
                self.waited[wk] = v
                e.wait_ge(sem, v)
            bi = ins.fn()
            if ins.is_dma:
                self.dcount[k] += 1
                ins.sem = self.dsems[k]
                ins.val = self.dcount[k] * 16
                bi.then_inc(self.dsems[k], 16)
            elif ins.signal:
                self.ecount[ins.eng] += 1
                ins.sem = self.esem[ins.eng]
                ins.val = self.ecount[ins.eng]
                bi.then_inc(self.esem[ins.eng], 1)
        engs = [SP] if final else [PE, ACT, DVE, POOL, SP]
        for en in engs:
            e = self.eng[en]
            for k in range(nds):
                if self.dcount[k] > 0:
                    key = (en, id(self.dsems[k]))
                    v = self.dcount[k] * 16
                    if self.waited.get(key, -1) < v:
                        self.waited[key] = v
                        e.wait_ge(self.dsems[k], v)
            for src in COMPUTE:
                v = self.ecount[src]
                if v > 0:
                    key = (en, id(self.esem[src]))
                    if self.waited.get(key, -1) < v:
                        self.waited[key] = v
                        e.wait_ge(self.esem[src], v)
        self.total += len(instrs)
        self.instrs = []
        for b in self.bufs:
            b.lastw = None
            b.readers = []


class KB:
    def __init__(self, n_stage=99, debug=False):
        self.n_stage = n_stage
        self.debug = debug
        self.nc = bass.Bass("TRN2", target_bir_lowering=False)
        self.gst = contextlib.ExitStack()
        self.P = Prog(self.nc, self.gst)
        self.din = {}
        self.uid = 0
        self.dbg_scan = False

    def sb(self, st, name, shape, dt):
        self.uid += 1
        return self.P.buf(name, st.enter_context(self.nc.sbuf_tensor("%s_%d" % (name, self.uid), shape, dt)))

    def ps(self, st, name, shape, dt):
        self.uid += 1
        return self.P.buf(name, st.enter_context(self.nc.psum_tensor("%s_%d" % (name, self.uid), shape, dt)))

    def dram_in(self, name, shape):
        t = self.nc.dram_tensor(name, list(shape), F32, kind="ExternalInput").ap()
        self.din[name] = t
        return t

    def dbg_dump(self, name, buf, shape, dt, ap=None):
        if not self.debug:
            return
        t = self.nc.dram_tensor(name, list(shape), dt, kind="ExternalOutput").ap()
        self.P.dma(SP, t, ap if ap is not None else buf[:], reads=[buf])

    def rsqrt(self, out_ap, x_ap, xbuf, obuf, t1, t2, yv, shape_sl, iters=3):
        nc, P = self.nc, self.P
        sl = shape_sl
        P.op(DVE, lambda: nc.vector.tensor_scalar(out=yv[sl].bitcast(I32), in0=x_ap.bitcast(I32), scalar1=1, scalar2=None,
                                                  op0=ALU.arith_shift_right), reads=[xbuf], writes=[yv])
        P.op(DVE, lambda: nc.vector.tensor_scalar(out=yv[sl].bitcast(I32), in0=yv[sl].bitcast(I32), scalar1=-1,
                                                  scalar2=0x5f3759df, op0=ALU.mult, op1=ALU.add), reads=[yv], writes=[yv])
        for it in range(iters):
            P.op(DVE, lambda: nc.vector.tensor_tensor(out=t1[sl], in0=yv[sl], in1=yv[sl], op=ALU.mult), reads=[yv], writes=[t1])
            P.op(DVE, lambda: nc.vector.scalar_tensor_tensor(out=t2[sl], in0=t1[sl], scalar=-0.5, in1=x_ap, op0=ALU.mult,
                                                             op1=ALU.mult), reads=[t1, xbuf], writes=[t2])
            last = it == iters - 1
            dst = out_ap if last else yv[sl]
            P.op(DVE, lambda dst=dst: nc.vector.scalar_tensor_tensor(out=dst, in0=t2[sl], scalar=1.5, in1=yv[sl], op0=ALU.add,
                                                                     op1=ALU.mult), reads=[t2, yv], writes=[obuf if last else yv])

    def rsqrt_pool(self, rstd, ss, n, scale):
        nc, P = self.nc, self.P
        P.op(POOL, lambda: nc.gpsimd.tensor_scalar(out=ss[:, 0:n], in0=ss[:, 0:n], scalar1=scale, scalar2=EPS, op0=ALU.mult, op1=ALU.add),
             reads=[ss], writes=[ss])
        P.op(POOL, lambda: nc.gpsimd.tensor_tensor(out=rstd[:, 0:n], in0=ss[:, 0:n], in1=self.neghalf[:, 0:n], op=ALU.pow),
             reads=[ss, self.neghalf], writes=[rstd])

    def load_w_bf16(self, st_stage, dst, dst_ap_fn, src_ap, nk, ncols, chunk=512, q=SP):
        nc, P = self.nc, self.P
        stg = self.stg
        srcv = src_ap.rearrange("(k p) n -> p k n", p=128)
        i = 0
        for k0 in range(0, nk, 8):
            k1 = min(nk, k0 + 8)
            for c0 in range(0, ncols, chunk):
                c1 = min(ncols, c0 + chunk)
                sg = stg[self.stg_i % len(stg)]
                self.stg_i += 1
                P.dma(q, sg[:, 0:k1 - k0, 0:c1 - c0], srcv[:, k0:k1, c0:c1], writes=[sg])
                eng = (ACT, DVE)[i % 2]
                i += 1
                o = dst_ap_fn(k0, k1, c0, c1)
                src = sg[:, 0:k1 - k0, 0:c1 - c0]
                if eng == POOL:
                    P.op(POOL, lambda o=o, src=src: nc.gpsimd.tensor_copy(out=o, in_=src), reads=[sg], writes=[dst])
                elif eng == ACT:
                    P.op(ACT, lambda o=o, src=src: nc.scalar.copy(out=o, in_=src), reads=[sg], writes=[dst])
                else:
                    P.op(DVE, lambda o=o, src=src: nc.vector.tensor_copy(out=o, in_=src), reads=[sg], writes=[dst])

    def bc_load(self, dst, src1d, q=SP):
        self.P.dma(q, dst[:], src1d.partition_broadcast(128), writes=[dst])

    def build(self):
        nc, P = self.nc, self.P
        gst = self.gst
        with gst:
            di = self.dram_in
            x_in = di("x", [NSEQ, D])
            ctx_in = di("ctx", [NCTX, D])
            cT = di("cT", [128, 8, 2])
            ada_w = di("ada_w", [DEPTH, D, 6 * D])
            ada_b = di("ada_b", [DEPTH, 6 * D])
            self.norms = {k: di(k, [DEPTH, D]) for k in ("norm_pre_mix", "norm_post_mix", "norm_pre_ffn", "norm_post_ffn")}
            self.cf = dict(w1=di("cf_w1", [2, D, 2 * D]), b1T=di("cf_b1T", [2, 128, 16]), dwT=di("cf_dwT", [2, 128, 8, KW]),
                           dwbT=di("cf_dwbT", [2, 128, 8]), lngT=di("cf_lngT", [2, 128, 8]), lnbT=di("cf_lnbT", [2, 128, 8]),
                           w2=di("cf_w2", [2, D, D]), b2=di("cf_b2", [2, D]))
            self.gl = dict(wq=di("gla_wq", [2, D, 512]), wk=di("gla_wk", [2, D, 512]), wv=di("gla_wv", [2, D, D]),
                           wr=di("gla_wr", [2, D, D]), wg1=di("gla_wg1c", [2, D, 32]), wg2=di("gla_wg2c", [2, 32, 1024]),
                           bg=di("gla_bgc", [2, 1024]), ng=di("gla_ngc", [2, 1024]), wo=di("gla_wo", [2, D, D]))
            self.ff = dict(wa=di("ffn_wa", [DEPTH, D, FF]), wb=di("ffn_wb", [DEPTH, D, FF]), dwT=di("ffn_dwT", [DEPTH, 128, NFC, 9]),
                           dwbT=di("ffn_dwbT", [DEPTH, 128, NFC]), wo=di("ffn_wo", [DEPTH, FF, D]))
            consts = di("consts", [128, 6, 128])
            out = nc.dram_tensor("out", [NSEQ, D], F32, kind="ExternalOutput").ap()
            dbg_ctx = nc.dram_tensor("dbg_ctx", [NCTX, D], F32, kind="ExternalOutput").ap() if self.debug else None
            self.mod = nc.dram_tensor("mod", [DEPTH, 2, 6 * D], F32, kind="Internal").ap()
            self.cdiag = nc.dram_tensor("cdiag", [8, 128, KW, 128], BF16, kind="Internal").ap()
            self.fdiag = nc.dram_tensor("fdiag", [NFC, 128, 9, 128], BF16, kind="Internal").ap()
            self.modT = nc.dram_tensor("modT", [DEPTH, 2, 128, 48], F32, kind="Internal").ap()
            self.npreT = {"mix": di("npre_mixT", [DEPTH, 128, 8]), "ffn": di("npre_ffnT", [DEPTH, 128, 8])}
            xs = [nc.dram_tensor("xs%d" % i, [NSEQ, D], F32, kind="Internal").ap() for i in range(2)]
            cs = [nc.dram_tensor("cs%d" % i, [NCTX, D], F32, kind="Internal").ap() for i in range(2)]
            gk = "ExternalOutput" if self.debug else "Internal"
            self.gscr = dict(
                qT=nc.dram_tensor("g_qT", [4, 128, NSEQ], F32, kind=gk).ap(),
                kT=nc.dram_tensor("g_kT", [4, 128, NSEQ], F32, kind=gk).ap(),
                v=nc.dram_tensor("g_v", [NSEQ, D], BF16, kind=gk).ap(),
                lb=nc.dram_tensor("g_lb", [NSEQ, 512], F32, kind=gk).ap(),
                of=nc.dram_tensor("g_of", [NSEQ, D], F32, kind=gk).ap(),
            )

            self.consts = consts
            self.identb = self.sb(gst, "identb", [128, 128], BF16)
            self.onesb = self.sb(gst, "onesb", [128, 128], BF16)
            self.maskfb = self.sb(gst, "maskfb", [128, 2, 128], BF16)
            self.bank = [self.ps(gst, "bank%d" % i, [128, 512], F32) for i in range(7)]
            self.pTb = self.ps(gst, "pTb", [128, 1024], BF16)
            self.junk = self.sb(gst, "junk", [128, D], BF16)
            self.neghalf = self.sb(gst, "neghalf", [128, 8], F32)
            P.op(POOL, lambda: nc.gpsimd.memset(self.neghalf[:], -0.5), writes=[self.neghalf])
            with contextlib.ExitStack() as stc:
                cst = self.sb(stc, "cst", [128, 6, 128], F32)
                P.dma(SP, cst[:], consts, writes=[cst])
                P.op(POOL, lambda: nc.gpsimd.tensor_copy(out=self.identb[:], in_=cst[:, 0, :]), reads=[cst], writes=[self.identb])
                P.op(POOL, lambda: nc.gpsimd.tensor_copy(out=self.onesb[:], in_=cst[:, 1, :]), reads=[cst], writes=[self.onesb])
                P.op(POOL, lambda: nc.gpsimd.tensor_copy(out=self.maskfb[:], in_=cst[:, 4:6, :]), reads=[cst], writes=[self.maskfb])
                P.flush()

            self.adaln_phase(cT, ada_w, ada_b)
            P.flush()

            stage = 0
            xcur, ccur = x_in, ctx_in
            xi, ci = 0, 0
            done = False
            for l in range(DEPTH):
                last = l == DEPTH - 1
                j = l // 2
                if stage >= self.n_stage:
                    break
                if l % 2 == 0:
                    with contextlib.ExitStack() as st:
                        self.conf_load(st, j)
                        P.flush()
                        xo = xs[xi]; xi ^= 1
                        self.conf_pass(st, l, j, xcur, xo, NSEQ, 0, 512)
                        xcur = xo
                        if not last:
                            co = cs[ci]; ci ^= 1
                            self.conf_pass(st, l, j, ccur, co, NCTX, 1, 256)
                            ccur = co
                        P.flush()
                else:
                    with contextlib.ExitStack() as st:
                        self.gla_load(st, j)
                        P.flush()
                        co = None
                        if not last:
                            co = cs[ci]; ci ^= 1
                        self.gla_seq(st, l, j, ccur, co, NCTX, 1, 256, zero_state=True, want_out=not last)
                        if co is not None:
                            ccur = co
                        xo = xs[xi]; xi ^= 1
                        self.gla_seq(st, l, j, xcur, xo, NSEQ, 0, 512, zero_state=False, want_out=True)
                        xcur = xo
                        P.flush()
                stage += 1
                if stage >= self.n_stage:
                    break
                with contextlib.ExitStack() as st:
                    self.ffn_load(st, l)
                    P.flush()
                    xo = out if last else xs[xi]
                    xi ^= 1
                    self.ffn_pass(st, l, xcur, xo, NSEQ, 0, 512, True)
                    xcur = xo
                    if not last:
                        co = cs[ci]; ci ^= 1
                        self.ffn_pass(st, l, ccur, co, NCTX, 1, 256, False)
                        ccur = co
                    P.flush()
                stage += 1
            if xcur is not out:
                with contextlib.ExitStack() as st:
                    tb = [self.sb(st, "cp%d" % i, [128, D], F32) for i in range(2)]
                    for t in range(NSEQ // 128):
                        b = tb[t % 2]
                        P.dma(SP, b[:], xcur[t * 128:(t + 1) * 128, :], writes=[b])
                        P.dma(SP, out[t * 128:(t + 1) * 128, :], b[:], reads=[b])
                    P.flush()
            if self.debug:
                with contextlib.ExitStack() as st:
                    tb = [self.sb(st, "cq%d" % i, [128, D], F32) for i in range(2)]
                    for t in range(NCTX // 128):
                        b = tb[t % 2]
                        P.dma(SP, b[:], ccur[t * 128:(t + 1) * 128, :], writes=[b])
                        P.dma(SP, dbg_ctx[t * 128:(t + 1) * 128, :], b[:], reads=[b])
                    P.flush()
            P.flush(final=True)
        return nc

    def adaln_phase(self, cT, ada_w, ada_b):
        nc, P = self.nc, self.P
        with contextlib.ExitStack() as st:
            sc = self.sb(st, "sc", [128, 8, 2], F32)
            wts = [self.sb(st, "adaw%d" % i, [128, 8, 512], F32) for i in range(3)]
            msb = self.sb(st, "msb", [2, 6 * D], F32)
            bb = self.sb(st, "bb", [2, 6 * D], F32)
            mTs = [self.sb(st, "mT%d" % i, [128, 2, 48], F32) for i in range(2)]
            self.identf = self.sb(st, "identf", [128, 128], F32)
            P.dma(SP, self.identf[:], self.consts[:, 0, :], writes=[self.identf])
            P.dma(SP, sc[:], cT, writes=[sc])
            P.op(ACT, lambda: nc.scalar.activation(out=sc[:], in_=sc[:], func=AF.Silu), reads=[sc], writes=[sc])
            it = 0
            for l in range(DEPTH):
                P.dma(SP, bb[:], ada_b[l].partition_broadcast(2), writes=[bb])
                wv = ada_w[l].rearrange("(k p) n -> p k n", p=128)
                for n in range(12):
                    wt = wts[it % 3]
                    pm = self.bank[it % 2]
                    it += 1
                    P.dma(SP if n % 2 == 0 else ACT, wt[:], wv[:, :, n * 512:(n + 1) * 512], writes=[wt])
                    for k in range(8):
                        P.op(PE, lambda k=k, wt=wt, pm=pm: nc.tensor.matmul(pm[0:2, :], lhsT=sc[:, k, :], rhs=wt[:, k, :],
                                                                           start=(k == 0), stop=(k == 7)),
                             reads=[sc, wt], writes=[pm])
                    P.op(DVE, lambda n=n, pm=pm: nc.vector.tensor_tensor(out=msb[:, n * 512:(n + 1) * 512], in0=pm[0:2, :],
                                                                       in1=bb[:, n * 512:(n + 1) * 512], op=ALU.add),
                         reads=[pm, bb], writes=[msb])
                P.dma(SP, self.mod[l], msb[:], reads=[msb])
                pX = self.bank[2 + (l % 2)]
                for cidx in range(48):
                    P.op(PE, lambda cidx=cidx, pX=pX: nc.tensor.transpose(out=pX[:, 2 * cidx:2 * cidx + 2], in_=msb[0:2, cidx * 128:(cidx + 1) * 128],
                                                                         identity=self.identf[0:2, 0:2]), reads=[msb, self.identf], writes=[pX])
                mT = mTs[l % 2]
                P.op(ACT, lambda pX=pX, mT=mT: nc.scalar.copy(out=mT[:], in_=pX[:, 0:96].rearrange("p (c s) -> p s c", s=2)), reads=[pX], writes=[mT])
                for s_ in range(2):
                    P.dma(SP, self.modT[l, s_], mT[:, s_, :], reads=[mT])
            P.flush()

    def mod_prep(self, st, l, s, which, tmp=None):
        nc, P = self.nc, self.P
        base = 0 if which == "mix" else 3 * D
        cb = 0 if which == "mix" else 24
        npost = self.norms["norm_post_mix" if which == "mix" else "norm_post_ffn"][l]
        AT = self.sb(st, "AT", [128, 8], F32)
        BT = self.sb(st, "BT", [128, 8], F32)
        G = self.sb(st, "G", [128, D], F32)
        with contextlib.ExitStack() as st2:
            own = tmp is None
            if own:
                tmp = self.sb(st2, "mtmp", [128, D], F32)
            tmp8 = self.sb(st if not own else st2, "mtmp8", [128, 8], F32)
            m = self.mod[l, s]
            mT = self.modT[l, s]
            P.dma(SP, BT[:], mT[:, cb:cb + 8], writes=[BT])
            P.dma(SP, AT[:], mT[:, cb + 8:cb + 16], writes=[AT])
            P.dma(SP, tmp8[:], self.npreT[which][l], writes=[tmp8])
            P.op(DVE, lambda: nc.vector.scalar_tensor_tensor(out=AT[:], in0=AT[:], scalar=1.0, in1=tmp8[:], op0=ALU.add, op1=ALU.mult),
                 reads=[AT, tmp8], writes=[AT])
            self.bc_load(G, m[base + 2 * D:base + 3 * D])
            self.bc_load(tmp, npost)
            P.op(DVE, lambda: nc.vector.tensor_tensor(out=G[:], in0=G[:], in1=tmp[:], op=ALU.mult), reads=[G, tmp], writes=[G])
            if own:
                P.flush()
        return AT, BT, G

    def prenorm_alloc(self, st, ntile_h, nbuf=2):
        d = {}
        d["xt"] = [self.sb(st, "xt%d" % i, [128, D], F32) for i in range(nbuf)]
        d["hb"] = [self.sb(st, "hb%d" % i, [128, D], BF16) for i in range(nbuf)]
        d["ss"] = [self.sb(st, "ss%d" % i, [128, 1], F32) for i in range(2)]
        d["r1"] = self.sb(st, "r1", [128, 1], F32)
        d["r2"] = self.sb(st, "r2", [128, 1], F32)
        d["r3"] = self.sb(st, "r3", [128, 1], F32)
        d["rstd"] = [self.sb(st, "rstd%d" % i, [128, 1], F32) for i in range(2)]
        d["hT"] = self.sb(st, "hT", [128, 8, ntile_h if ntile_h > 64 else ntile_h * 128], BF16)
        d["i"] = 0
        d["li"] = 0
        return d

    def prenorm_tile(self, pn, x_src, j, pos, AT, BT, pT, cmap=None):
        tok = self.prenorm_A(pn, x_src, j, pos, cmap)
        self.prenorm_B(pn, tok, AT, BT, pT)

    def prenorm_many(self, pn, x_src, jlist, AT, BT, pT):
        n = len(jlist)
        lt, toks = {}, {}
        for i in range(min(2, n)):
            lt[i] = self.prenorm_load(pn, x_src, jlist[i][0])
        if n:
            toks[0] = self.prenorm_stat(pn, lt[0], jlist[0][1])
        for i in range(n):
            if i + 2 < n:
                lt[i + 2] = self.prenorm_load(pn, x_src, jlist[i + 2][0])
            if i + 1 < n:
                toks[i + 1] = self.prenorm_stat(pn, lt[i + 1], jlist[i + 1][1])
            self.prenorm_B(pn, toks[i], AT, BT, pT)

    def prenorm_load(self, pn, x_src, j, q=SP):
        if j is None:
            return None
        nb = len(pn["xt"])
        i = pn["li"]
        pn["li"] += 1
        xt = pn["xt"][i % nb]
        self.P.dma(q, xt[:], x_src[j * 128:(j + 1) * 128, :], writes=[xt])
        return i

    def prenorm_stat(self, pn, ltok, pos, cmap=None):
        nc, P = self.nc, self.P
        if cmap is None:
            cmap = (pos * 128, 0, 128)
        if ltok is None:
            return (None, cmap)
        nb = len(pn["xt"])
        xt = pn["xt"][ltok % nb]
        hb = pn["hb"][ltok % len(pn["hb"])]
        ss = pn["ss"][ltok % 2]
        rstd = pn["rstd"][ltok % 2]
        junk = self.junk
        P.op(ACT, lambda: nc.scalar.activation(out=junk[:], in_=xt[:], func=AF.Square, accum_out=ss[:]), reads=[xt], writes=[junk, ss])
        self.rsqrt_pool(rstd, ss, 1, 1.0 / D)
        P.op(DVE, lambda: nc.vector.tensor_scalar(out=hb[:], in0=xt[:], scalar1=rstd[:, 0:1], scalar2=None, op0=ALU.mult),
             reads=[xt, rstd], writes=[hb])
        return (hb, cmap)

    def prenorm_A(self, pn, x_src, j, pos, cmap=None, q=None):
        lt = self.prenorm_load(pn, x_src, j, q=q if q is not None else SP)
        return self.prenorm_stat(pn, lt, pos, cmap)

    def prenorm_B(self, pn, tok, AT, BT, pT):
        nc, P = self.nc, self.P
        hT = pn["hT"]
        hb, (dcol, slo, shi) = tok
        if hb is None:
            P.op(POOL, lambda: nc.gpsimd.memset(hT[:, :, dcol:dcol + (shi - slo)], 0.0), writes=[hT])
            return
        for k in range(8):
            P.op(PE, lambda k=k: nc.tensor.transpose(out=pT[:, k * 128:(k + 1) * 128], in_=hb[:, k * 128:(k + 1) * 128],
                                                     identity=self.identb[:]), reads=[hb, self.identb], writes=[pT])
        for k in range(8):
            P.op(ACT, lambda k=k: nc.scalar.activation(out=hT[:, k, dcol:dcol + (shi - slo)], in_=pT[:, k * 128 + slo:k * 128 + shi],
                                                       func=AF.Identity, bias=BT[:, k:k + 1], scale=AT[:, k:k + 1]),
                 reads=[pT, AT, BT], writes=[hT])

    def post_alloc(self, st):
        d = {}
        d["xr"] = [self.sb(st, "xr%d" % i, [128, D], F32) for i in range(2)]
        d["y"] = [self.sb(st, "ysb%d" % i, [128, D], F32) for i in range(1)]
        d["sq"] = self.junk
        d["ss"] = [self.sb(st, "pss%d" % i, [128, 1], F32) for i in range(2)]
        d["rstd"] = [self.sb(st, "prstd%d" % i, [128, 1], F32) for i in range(2)]
        d["r1"] = self.sb(st, "pr1", [128, 1], F32)
        d["r2"] = self.sb(st, "pr2", [128, 1], F32)
        d["r3"] = self.sb(st, "pr3", [128, 1], F32)
        d["i"] = 0
        return d

    def post_tile(self, po, pys, x_src, x_dst, j, G, bias_bc=None, next_j=None):
        nc, P = self.nc, self.P
        i = po["i"]
        po["i"] += 1
        xr = po["xr"][i % 2]
        y = po["y"][0]
        ss = po["ss"][i % 2]
        rstd = po["rstd"][i % 2]
        sq = po["sq"]
        if po.get("loaded") != j:
            P.dma(SP, xr[:], x_src[j * 128:(j + 1) * 128, :], writes=[xr])
        for h in range(2):
            if bias_bc is not None:
                P.op(DVE, lambda h=h: nc.vector.tensor_tensor(out=y[:, h * 512:(h + 1) * 512], in0=pys[h][:],
                                                             in1=bias_bc[:, h * 512:(h + 1) * 512], op=ALU.add),
                     reads=[pys[h], bias_bc], writes=[y])
            else:
                P.op(ACT, lambda h=h: nc.scalar.copy(out=y[:, h * 512:(h + 1) * 512], in_=pys[h][:]), reads=[pys[h]], writes=[y])
        P.op(ACT, lambda: nc.scalar.activation(out=sq[:], in_=y[:], func=AF.Square, accum_out=ss[:]), reads=[y], writes=[sq, ss])
        self.rsqrt_pool(rstd, ss, 1, 1.0 / D)
        P.op(DVE, lambda: nc.vector.scalar_tensor_tensor(out=y[:], in0=y[:], scalar=rstd[:, 0:1], in1=G[:], op0=ALU.mult, op1=ALU.mult),
             reads=[y, rstd, G], writes=[y])
        P.op(DVE, lambda: nc.vector.tensor_tensor(out=xr[:], in0=xr[:], in1=y[:], op=ALU.add), reads=[xr, y], writes=[xr])
        if next_j is not None:
            xn = po["xr"][(i + 1) % 2]
            P.dma(SP, xn[:], x_src[next_j * 128:(next_j + 1) * 128, :], writes=[xn])
            po["loaded"] = next_j
        P.dma(POOL, x_dst[j * 128:(j + 1) * 128, :], xr[:], reads=[xr])

    def conf_load(self, st, j):
        nc, P = self.nc, self.P
        cf = self.cf
        w = {}
        w["w1"] = self.sb(st, "w1b", [128, 8, 2 * D], BF16)
        w["w2"] = self.sb(st, "w2b", [128, 8, D], BF16)
        w["b1T"] = self.sb(st, "b1T", [128, 16], F32)
        w["hb1g"] = self.sb(st, "hb1g", [128, 8], F32)
        w["dwT"] = self.sb(st, "dwT", [128, 8, KW], F32)
        w["dwbT"] = self.sb(st, "dwbT", [128, 8], F32)
        w["lngT"] = self.sb(st, "lngT", [128, 8], F32)
        w["lnbT"] = self.sb(st, "lnbT", [128, 8], F32)
        w["b2"] = self.sb(st, "b2bc", [128, D], F32)
        self.cw = w
        with contextlib.ExitStack() as st2:
            self.stg = [self.sb(st2, "stg%d" % i, [128, 8, 512], F32) for i in range(3)]
            self.stg_i = 0
            self.conv_all = True
            self.load_w_bf16(st2, w["w1"], lambda k0, k1, c0, c1: w["w1"][:, k0:k1, c0:c1], cf["w1"][j], 8, 2 * D)
            self.load_w_bf16(st2, w["w2"], lambda k0, k1, c0, c1: w["w2"][:, k0:k1, c0:c1], cf["w2"][j], 8, D)
            P.dma(SP, w["b1T"][:], cf["b1T"][j], writes=[w["b1T"]])
            P.dma(SP, w["dwT"][:], cf["dwT"][j], writes=[w["dwT"]])
            P.dma(SP, w["dwbT"][:], cf["dwbT"][j], writes=[w["dwbT"]])
            P.dma(SP, w["lngT"][:], cf["lngT"][j], writes=[w["lngT"]])
            P.dma(SP, w["lnbT"][:], cf["lnbT"][j], writes=[w["lnbT"]])
            self.bc_load(w["b2"], cf["b2"][j])
            P.op(DVE, lambda: nc.vector.tensor_scalar(out=w["hb1g"][:], in0=w["b1T"][:, 8:16], scalar1=0.5, scalar2=None, op0=ALU.mult),
                 reads=[w["b1T"]], writes=[w["hb1g"]])
            P.op(DVE, lambda: nc.vector.tensor_scalar(out=w["dwT"][:], in0=w["dwT"][:], scalar1=0.5, scalar2=None, op0=ALU.mult),
                 reads=[w["dwT"]], writes=[w["dwT"]])
            dgb = [self.sb(st2, "dgb%d" % i, [128, KW, 128], BF16) for i in range(2)]
            for c in range(8):
                dg = dgb[c % 2]
                for tap in range(KW):
                    P.op(DVE, lambda c=c, tap=tap, dg=dg: nc.vector.tensor_scalar(
                        out=dg[:, tap, :], in0=self.identb[:], scalar1=w["dwT"][:, c, tap:tap + 1], scalar2=None, op0=ALU.mult),
                        reads=[self.identb, w["dwT"]], writes=[dg])
                P.dma(SP, self.cdiag[c], dg[:], reads=[dg])
            P.flush()

    def conf_pass(self, st0, l, j, x_src, x_dst, N, s, T):
        nc, P = self.nc, self.P
        w = self.cw
        HAL = 15
        TW = T + 2 * HAL
        nt_c = T // 128
        nt_h = nt_c + 2
        off = 128 - HAL
        half = TW // 2
        assert TW % 2 == 0 and half <= 512
        bank = self.bank
        pa = (bank[0], bank[1]); pg = (bank[2], bank[3]); pc = bank[4]; pT = self.pTb; py = (bank[5], bank[6])
        with contextlib.ExitStack() as st:
            po = self.post_alloc(st)
            AT, BT, G = self.mod_prep(st, l, s, "mix", tmp=po["y"][0])
            pn = self.prenorm_alloc(st, nt_h, nbuf=4)
            hT = pn["hT"]
            glu = self.sb(st, "glu", [128, 8, TW], BF16)
            tg = [self.sb(st, "tg%d" % i, [128, 2, half], F32) for i in range(2)]
            asb = [self.sb(st, "asb%d" % i, [128, 2, half], F32) for i in range(2)]
            diag = [self.sb(st, "diag%d" % i, [128, KW, 128], BF16) for i in range(3)]
            vsb = self.sb(st, "vsb", [128, 8, T], F32)
            vb = self.sb(st, "vb", [128, 8, T], BF16)
            vsq = self.sb(st, "vsq", [128, 8, T], BF16)
            mean = self.sb(st, "mean", [128, T], F32)
            m2 = self.sb(st, "m2", [128, T], F32)
            var = self.sb(st, "var", [128, T], F32)
            rs = self.sb(st, "rs", [128, T], F32)
            q1 = self.sb(st, "q1", [128, T], F32)
            q2 = self.sb(st, "q2", [128, T], F32)
            q3 = self.sb(st, "q3", [128, T], F32)
            vn = [self.sb(st, "vn%d" % i, [128, T], F32) for i in range(2)]
            uT = self.sb(st, "uT", [128, 8, T], BF16)
            nsteps = N // T
            ntiles = N // 128
            dctr = [0]

            def pre(t, pos, part="AB", tok=None):
                jt = t * nt_c - 1 + pos
                jj = jt if 0 <= jt < ntiles else None
                if part == "AB":
                    self.prenorm_tile(pn, x_src, jj, pos, AT, BT, pT)
                elif part == "L":
                    return self.prenorm_load(pn, x_src, jj)
                elif part == "S":
                    return self.prenorm_stat(pn, tok, pos)
                else:
                    self.prenorm_B(pn, tok, AT, BT, pT)

            def w1glu(t, c):
                tgc = tg[c % 2]; ac = asb[c % 2]
                for hf in range(2):
                    for k in range(8):
                        P.op(PE, lambda k=k, hf=hf, c=c: nc.tensor.matmul(
                            pa[hf][:, 0:half], lhsT=w["w1"][:, k, c * 128:(c + 1) * 128],
                            rhs=hT[:, k, off + hf * half: off + (hf + 1) * half], start=(k == 0), stop=(k == 7)),
                            reads=[w["w1"], hT], writes=[pa[hf]])
                for hf in range(2):
                    for k in range(8):
                        P.op(PE, lambda k=k, hf=hf, c=c: nc.tensor.matmul(
                            pg[hf][:, 0:half], lhsT=w["w1"][:, k, D + c * 128: D + (c + 1) * 128],
                            rhs=hT[:, k, off + hf * half: off + (hf + 1) * half], start=(k == 0), stop=(k == 7)),
                            reads=[w["w1"], hT], writes=[pg[hf]])
                for hf in range(2):
                    P.op(ACT, lambda hf=hf, c=c, tgc=tgc: nc.scalar.activation(
                        out=tgc[:, hf, :], in_=pg[hf][:, 0:half], func=AF.Tanh, bias=w["hb1g"][:, c:c + 1], scale=0.5),
                        reads=[pg[hf], w["hb1g"]], writes=[tgc])
                    P.op(ACT, lambda hf=hf, c=c, ac=ac: nc.scalar.activation(
                        out=ac[:, hf, :], in_=pa[hf][:, 0:half], func=AF.Identity, bias=w["b1T"][:, c:c + 1], scale=1.0),
                        reads=[pa[hf], w["b1T"]], writes=[ac])
                P.op(DVE, lambda c=c, tgc=tgc, ac=ac: nc.vector.scalar_tensor_tensor(
                    out=glu[:, c, :], in0=tgc[:].rearrange("p a b -> p (a b)"), scalar=1.0,
                    in1=ac[:].rearrange("p a b -> p (a b)"), op0=ALU.add, op1=ALU.mult),
                    reads=[tgc, ac], writes=[glu])
                if t == 0:
                    P.op(POOL, lambda c=c: nc.gpsimd.memset(glu[:, c, 0:HAL], 0.0), writes=[glu])
                if t == nsteps - 1:
                    P.op(POOL, lambda c=c: nc.gpsimd.memset(glu[:, c, TW - HAL:TW], 0.0), writes=[glu])

            def diag_load(n):
                if n < nsteps * 8:
                    dg_ = diag[n % 3]
                    P.dma(SP, dg_[:], self.cdiag[n % 8], writes=[dg_])

            def conv(t, c):
                n = t * 8 + c
                dg = diag[n % 3]
                if n == 0:
                    for m in range(3):
                        diag_load(m)
                for tap in range(KW):
                    P.op(PE, lambda c=c, tap=tap, dg=dg: nc.tensor.matmul(
                        pc[:, 0:T], lhsT=dg[:, tap, :], rhs=glu[:, c, tap:tap + T], start=(tap == 0), stop=(tap == KW - 1)),
                        reads=[dg, glu], writes=[pc])
                diag_load(n + 3)
                P.op(ACT, lambda c=c: nc.scalar.activation(out=vsb[:, c, :], in_=pc[:, 0:T], func=AF.Identity,
                                                           bias=w["dwbT"][:, c:c + 1], scale=1.0),
                     reads=[pc, w["dwbT"]], writes=[vsb])
                P.op(DVE, lambda c=c: nc.vector.tensor_copy(out=vb[:, c, :], in_=vsb[:, c, :]), reads=[vsb], writes=[vb])
                P.op(ACT, lambda c=c: nc.scalar.activation(out=vsq[:, c, :], in_=vsb[:, c, :], func=AF.Square),
                     reads=[vsb], writes=[vsq])

            def stats(t):
                ps1, ps2 = pa[0], pg[0]
                for c in range(8):
                    P.op(PE, lambda c=c: nc.tensor.matmul(ps1[:, 0:T], lhsT=self.onesb[:], rhs=vb[:, c, :], start=(c == 0), stop=(c == 7)),
                         reads=[self.onesb, vb], writes=[ps1])
                for c in range(8):
                    P.op(PE, lambda c=c: nc.tensor.matmul(ps2[:, 0:T], lhsT=self.onesb[:], rhs=vsq[:, c, :], start=(c == 0), stop=(c == 7)),
                         reads=[self.onesb, vsq], writes=[ps2])
                P.op(ACT, lambda: nc.scalar.copy(out=mean[:], in_=ps1[:, 0:T]), reads=[ps1], writes=[mean])
                P.op(DVE, lambda: nc.vector.tensor_tensor(out=m2[:], in0=mean[:], in1=mean[:], op=ALU.mult), reads=[mean], writes=[m2])
                P.op(DVE, lambda: nc.vector.scalar_tensor_tensor(out=var[:], in0=ps2[:, 0:T], scalar=EPS, in1=m2[:], op0=ALU.add,
                                                                 op1=ALU.subtract), reads=[ps2, m2], writes=[var])
                self.rsqrt(rs[:], var[:], var, rs, q1, q2, q3, slice(None), iters=2)

            def normalize(t, c):
                v_ = vn[c % 2]
                P.op(DVE, lambda c=c, v_=v_: nc.vector.tensor_tensor(out=v_[:], in0=vsb[:, c, :], in1=mean[:], op=ALU.subtract),
                     reads=[vsb, mean], writes=[v_])
                P.op(DVE, lambda c=c, v_=v_: nc.vector.tensor_tensor(out=v_[:], in0=v_[:], in1=rs[:], op=ALU.mult),
                     reads=[v_, rs], writes=[v_])
                P.op(ACT, lambda c=c, v_=v_: nc.scalar.activation(out=uT[:, c, :], in_=v_[:], func=AF.Silu,
                                                                  bias=w["lnbT"][:, c:c + 1], scale=w["lngT"][:, c:c + 1]),
                     reads=[v_, w["lnbT"], w["lngT"]], writes=[uT])

            def w2post(t, tt):
                for h in range(2):
                    for c in range(8):
                        P.op(PE, lambda c=c, h=h, tt=tt: nc.tensor.matmul(
                            py[h][:], lhsT=uT[:, c, tt * 128:(tt + 1) * 128], rhs=w["w2"][:, c, h * 512:(h + 1) * 512],
                            start=(c == 0), stop=(c == 7)), reads=[uT, w["w2"]], writes=[py[h]])
                jn = t * nt_c + tt + 1
                self.post_tile(po, py, x_src, x_dst, t * nt_c + tt, G, bias_bc=w["b2"], next_j=jn if jn < ntiles else None)

            def pre_pipeline(t, bodies):
                LA = len(pn["xt"]) - 1
                lt, tk = {}, {}
                for i in range(min(LA, nt_h)):
                    lt[i] = pre(t, i, "L")
                tk[0] = pre(t, 0, "S", lt[0])
                for i in range(max(nt_h, len(bodies))):
                    if i + LA < nt_h:
                        lt[i + LA] = pre(t, i + LA, "L")
                    if i + 1 < nt_h:
                        tk[i + 1] = pre(t, i + 1, "S", lt[i + 1])
                    if i < len(bodies):
                        bodies[i]()
                    if i < nt_h:
                        pre(t, i, "B", tk[i])

            pre_pipeline(0, [])
            for c in range(8):
                w1glu(0, c)
            for t in range(nsteps):
                for c in range(8):
                    conv(t, c)
                stats(t)
                nxt = t + 1 < nsteps
                bodies = [(lambda i=i: normalize(t, i)) for i in range(8)]
                if nxt:
                    pre_pipeline(t + 1, bodies)
                else:
                    for b in bodies:
                        b()
                for i in range(8):
                    if nxt:
                        w1glu(t + 1, i)
                    if nt_c == 4:
                        if i % 2 == 1:
                            w2post(t, i // 2)
                    elif i < nt_c:
                        w2post(t, i)
            P.flush()

    def ffn_load(self, st, l):
        nc, P = self.nc, self.P
        ff = self.ff
        w = {}
        w["wa"] = self.sb(st, "wab", [128, 8, FF], BF16)
        w["wb"] = self.sb(st, "wbb", [128, 8, FF], BF16)
        w["wo"] = self.sb(st, "wob", [128, NFC, D], BF16)
        w["dwT"] = self.sb(st, "fdwT", [128, NFC, 9], F32)
        w["dwbT"] = self.sb(st, "fdwbT", [128, NFC], F32)
        self.fw = w
        with contextlib.ExitStack() as st2:
            self.stg = [self.sb(st2, "stg%d" % i, [128, 8, 512], F32) for i in range(3)]
            self.stg_i = 0
            self.conv_all = True
            self.load_w_bf16(st2, w["wa"], lambda k0, k1, c0, c1: w["wa"][:, k0:k1, c0:c1], ff["wa"][l], 8, FF)
            self.load_w_bf16(st2, w["wb"], lambda k0, k1, c0, c1: w["wb"][:, k0:k1, c0:c1], ff["wb"][l], 8, FF)
            self.load_w_bf16(st2, w["wo"], lambda k0, k1, c0, c1: w["wo"][:, k0:k1, c0:c1], ff["wo"][l], NFC, D)
            P.dma(SP, w["dwT"][:], ff["dwT"][l], writes=[w["dwT"]])
            P.dma(SP, w["dwbT"][:], ff["dwbT"][l], writes=[w["dwbT"]])
            dgb = [self.sb(st2, "fdgb%d" % i, [128, 9, 128], BF16) for i in range(2)]
            for c in range(NFC):
                dg = dgb[c % 2]
                for tap in range(9):
                    P.op(DVE, lambda c=c, tap=tap, dg=dg: nc.vector.tensor_scalar(
                        out=dg[:, tap, :], in0=self.identb[:], scalar1=w["dwT"][:, c, tap:tap + 1], scalar2=None, op0=ALU.mult),
                        reads=[self.identb, w["dwT"]], writes=[dg])
                P.dma(SP, self.fdiag[c], dg[:], reads=[dg])
            P.flush()

    def ffn_pass(self, st0, l, x_src, x_dst, N, s, T, grid):
        nc, P = self.nc, self.P
        w = self.fw
        bank = self.bank
        nt_c = T // 128
        if grid:
            W, R, hal = 64, T // 64, 1
            taps = [(dr, dc, (dr + 1) * 3 + (dc + 1)) for dr in (-1, 0, 1) for dc in (-1, 0, 1)]
        else:
            W, R, hal = T, 1, 0
            taps = [(0, dc, 3 + (dc + 1)) for dc in (-1, 0, 1)]
        nt_h = nt_c + 2 * hal
        RA = R + 2 * hal
        acols = RA * W
        aoff = 0
        goff = hal * 64
        hcols = T + 2 * hal * 64
        nsplit = 2 if acols > 512 else 1
        rows_sp = RA // nsplit
        csp = rows_sp * W
        assert rows_sp * nsplit == RA and csp <= 512
        ntap = len(taps)
        pa = (bank[0], bank[1]); pgs = (bank[2], bank[3]); pcs = (bank[4], bank[5]); pT = self.pTb; py = (bank[6], bank[0])
        ntiles = N // 128
        with contextlib.ExitStack() as st:
            po = self.post_alloc(st)
            AT, BT, G = self.mod_prep(st, l, s, "ffn", tmp=po["y"][0])
            pn = self.prenorm_alloc(st, hcols)
            hT = pn["hT"]
            apad = [self.sb(st, "apad%d" % i, [128, RA, W + 2], BF16) for i in range(2)]
            ssb = [self.sb(st, "ssb%d" % i, [128, T], BF16) for i in range(2)]
            NFD = 2
            fdg = [self.sb(st, "fdg%d" % i, [128, 9, 128], BF16) for i in range(NFD)]
            KDVE = 4 if ntap == 9 else 0
            acc = [self.sb(st, "acc%d" % i, [128, T], F32) for i in range(1)] if KDVE else None
            u = self.sb(st, "u", [128, NFC, T], BF16)
            for ap_ in apad:
                P.op(POOL, lambda ap_=ap_: nc.gpsimd.memset(ap_[:], 0.0), writes=[ap_])
            nsteps = N // T

            def prenorm_pos(t, pos, part="AB", tok=None):
                jt = t * nt_c - hal + pos
                if hal == 0:
                    cm = (pos * 128, 0, 128)
                elif pos == 0:
                    cm = (0, 64, 128)
                elif pos == nt_h - 1:
                    cm = (64 + nt_c * 128, 0, 64)
                else:
                    cm = (64 + (pos - 1) * 128, 0, 128)
                jj = jt if 0 <= jt < ntiles else None
                if part == "AB":
                    self.prenorm_tile(pn, x_src, jj, pos, AT, BT, pT, cmap=cm)
                elif part == "L":
                    return self.prenorm_load(pn, x_src, jj)
                elif part == "S":
                    return self.prenorm_stat(pn, tok, pos, cmap=cm)
                else:
                    self.prenorm_B(pn, tok, AT, BT, pT)

            def ab_mm(c):
                ap_ = apad[c % 2]; pg = pgs[c % 2]
                for sp in range(nsplit):
                    for k in range(8):
                        P.op(PE, lambda k=k, sp=sp, c=c: nc.tensor.matmul(
                            pa[sp][:, 0:csp], lhsT=w["wa"][:, k, c * 128:(c + 1) * 128],
                            rhs=hT[:, k, aoff + sp * csp: aoff + (sp + 1) * csp], start=(k == 0), stop=(k == 7)),
                            reads=[w["wa"], hT], writes=[pa[sp]])
                for k in range(8):
                    P.op(PE, lambda k=k, c=c, pg=pg: nc.tensor.matmul(
                        pg[:, 0:T], lhsT=w["wb"][:, k, c * 128:(c + 1) * 128],
                        rhs=hT[:, k, goff: goff + T], start=(k == 0), stop=(k == 7)),
                        reads=[w["wb"], hT], writes=[pg])
                for sp in range(nsplit):
                    P.op(ACT, lambda sp=sp, ap_=ap_: nc.scalar.copy(
                        out=ap_[:, sp * rows_sp:(sp + 1) * rows_sp, 1:W + 1],
                        in_=pa[sp][:, 0:csp].rearrange("p (r w) -> p r w", w=W)), reads=[pa[sp]], writes=[ap_])

            dctr = [0]

            def fdiag_load(n):
                if n < nsteps * NFC:
                    dg_ = fdg[n % NFD]
                    P.dma(SP, dg_[:], self.fdiag[n % NFC], writes=[dg_])

            def conv(c):
                ap_ = apad[c % 2]; pg = pgs[c % 2]; pc = pcs[c % 2]; sb_ = ssb[c % 2]
                n = dctr[0]
                dctr[0] += 1
                dg = fdg[n % NFD]
                if n == 0:
                    for m in range(NFD):
                        fdiag_load(m)
                pcv = pc[:, 0:T].rearrange("p (r w) -> p r w", w=W)
                n_pe = ntap - KDVE
                pe_taps = taps[:n_pe]; dve_taps = taps[n_pe:]
                ac_ = acc[0] if KDVE else None
                if KDVE:
                    accv = ac_[:].rearrange("p (r w) -> p r w", w=W)
                    for ti, (dr, dc, wi) in enumerate(dve_taps):
                        src = ap_[:, hal + dr: hal + dr + R, 1 + dc: 1 + dc + W]
                        if ti == 0:
                            P.op(DVE, lambda src=src, accv=accv, c=c, wi=wi: nc.vector.tensor_scalar(
                                out=accv, in0=src, scalar1=w["dwT"][:, c, wi:wi + 1], scalar2=None, op0=ALU.mult),
                                reads=[ap_, w["dwT"]], writes=[ac_])
                        else:
                            P.op(DVE, lambda src=src, accv=accv, c=c, wi=wi: nc.vector.scalar_tensor_tensor(
                                out=accv, in0=src, scalar=w["dwT"][:, c, wi:wi + 1], in1=accv, op0=ALU.mult, op1=ALU.add),
                                reads=[ap_, w["dwT"], ac_], writes=[ac_])
                for ti, (dr, dc, wi) in enumerate(pe_taps):
                    src = ap_[:, hal + dr: hal + dr + R, 1 + dc: 1 + dc + W]
                    P.op(PE, lambda src=src, pcv=pcv, dg=dg, wi=wi, ti=ti: nc.tensor.matmul(
                        pcv, lhsT=dg[:, wi, :], rhs=src, start=(ti == 0), stop=(ti == n_pe - 1)),
                        reads=[dg, ap_], writes=[pc])
                fdiag_load(n + NFD)
                if KDVE:
                    P.op(DVE, lambda ac_=ac_, pc=pc: nc.vector.tensor_tensor(out=ac_[:], in0=ac_[:], in1=pc[:, 0:T], op=ALU.add),
                         reads=[ac_, pc], writes=[ac_])
                    P.op(ACT, lambda c=c, ac_=ac_, sb_=sb_: nc.scalar.activation(out=sb_[:], in_=ac_[:], func=AF.Silu,
                                                                               bias=w["dwbT"][:, c:c + 1], scale=1.0),
                         reads=[ac_, w["dwbT"]], writes=[sb_])
                else:
                    P.op(ACT, lambda c=c, pc=pc, sb_=sb_: nc.scalar.activation(out=sb_[:], in_=pc[:, 0:T], func=AF.Silu,
                                                                             bias=w["dwbT"][:, c:c + 1], scale=1.0),
                         reads=[pc, w["dwbT"]], writes=[sb_])
                P.op(DVE, lambda c=c, sb_=sb_, pg=pg: nc.vector.tensor_tensor(out=u[:, c, :], in0=sb_[:], in1=pg[:, 0:T], op=ALU.mult),
                     reads=[sb_, pg], writes=[u])

            def wo_mm(t, tt):
                for h in range(2):
                    for c in range(NFC):
                        P.op(PE, lambda c=c, h=h, tt=tt: nc.tensor.matmul(
                            py[h][:], lhsT=u[:, c, tt * 128:(tt + 1) * 128], rhs=w["wo"][:, c, h * 512:(h + 1) * 512],
                            start=(c == 0), stop=(c == NFC - 1)), reads=[u, w["wo"]], writes=[py[h]])

            def wo_post(t, tt):
                jn = t * nt_c + tt + 1
                self.post_tile(po, py, x_src, x_dst, t * nt_c + tt, G, next_j=jn if jn < ntiles else None)

            def pre_pipeline(t, pre_bodies, post_bodies):
                lt, tk = {}, {}
                for i in range(min(1, nt_h)):
                    lt[i] = prenorm_pos(t, i, "L")
                tk[0] = prenorm_pos(t, 0, "S", lt[0])
                for i in range(max(nt_h, len(pre_bodies))):
                    if i + 1 < nt_h:
                        lt[i + 1] = prenorm_pos(t, i + 1, "L")
                        tk[i + 1] = prenorm_pos(t, i + 1, "S", lt[i + 1])
                    if i < len(pre_bodies):
                        pre_bodies[i]()
                    if i < nt_h:
                        prenorm_pos(t, i, "B", tk[i])
                    if i < len(post_bodies):
                        post_bodies[i]()

            pre_pipeline(0, [], [])
            for t in range(nsteps):
                for c in range(NFC + 1):
                    if c < NFC:
                        ab_mm(c)
                    if c >= 1:
                        conv(c - 1)
                nxt = t + 1 < nsteps
                mm_b = [(lambda i=i: wo_mm(t, i)) for i in range(nt_c)]
                po_b = [(lambda i=i: wo_post(t, i)) for i in range(nt_c)]
                if nxt:
                    pre_pipeline(t + 1, mm_b, po_b)
                else:
                    for i in range(nt_c):
                        mm_b[i]()
                        po_b[i]()
            P.flush()

    def gla_load(self, st, j):
        nc, P = self.nc, self.P
        gl = self.gl
        w = {}
        w["wq"] = self.sb(st, "wqb", [128, 8, 512], BF16)
        w["wk"] = self.sb(st, "wkb", [128, 8, 512], BF16)
        w["wv"] = self.sb(st, "wvb", [128, 8, D], BF16)
        w["wr"] = self.sb(st, "wrb", [128, 8, D], BF16)
        w["wo"] = self.sb(st, "wob", [128, 8, D], BF16)
        w["wg1"] = self.sb(st, "wg1b", [128, 8, 32], BF16)
        w["wg2"] = self.sb(st, "wg2b", [32, D], BF16)
        w["bg"] = self.sb(st, "bgbc", [128, D], F32)
        w["ng"] = self.sb(st, "ngbc", [128, D], F32)
        self.S = [[self.sb(st, "S%d_%d" % (d_, h), [128, 256], F32) for h in range(4)] for d_ in range(2)]
        self.tri = self.sb(st, "tri", [128, 2, 128], F32)
        P.dma(SP, self.tri[:], self.consts[:, 2:4, :], writes=[self.tri])
        self.gw = w
        with contextlib.ExitStack() as st2:
            self.stg = [self.sb(st2, "stg%d" % i, [128, 8, 512], F32) for i in range(3)]
            self.stg_i = 0
            self.conv_all = True
            for nm, ncol in (("wq", 512), ("wk", 512), ("wv", D), ("wr", D), ("wo", D), ("wg1", 32)):
                self.load_w_bf16(st2, w[nm], lambda k0, k1, c0, c1, nm=nm: w[nm][:, k0:k1, c0:c1], gl[nm][j], 8, ncol)
            g2f = self.sb(st2, "g2f", [32, D], F32)
            P.dma(SP, g2f[:], gl["wg2"][j], writes=[g2f])
            P.op(DVE, lambda: nc.vector.tensor_copy(out=w["wg2"][:], in_=g2f[:]), reads=[g2f], writes=[w["wg2"]])
            self.bc_load(w["bg"], gl["bg"][j])
            self.bc_load(w["ng"], gl["ng"][j])
            P.flush()

    def gla_prep(self, st, d_, T, lsb, qT_sb, kT_sb, bufs):
        nc, P = self.nc, self.P
        nt_c = T // 128
        nch = T // 64
        khT, qhT, edec, Dsbs, EKs, EQs, Lends = bufs
        bank = self.bank
        endcol = 63 if d_ == 0 else 0
        for h in range(4):
            LT = bank[5 + (h % 2)]
            Dsb, EK, EQ, Lend = Dsbs[h % 2], EKs[h % 2], EQs[h % 2], Lends[h % 2]
            for tt in range(nt_c):
                P.op(PE, lambda tt=tt, h=h, LT=LT: nc.tensor.matmul(LT[:, tt * 128:(tt + 1) * 128], lhsT=lsb[:, tt, h * 128:(h + 1) * 128],
                                                                  rhs=self.tri[:, d_, :], start=True, stop=True),
                     reads=[lsb, self.tri], writes=[LT])
            LTv = LT[:, 0:T].rearrange("p (c j) -> p c j", j=64)
            P.op(ACT, lambda LTv=LTv, Lend=Lend: nc.scalar.copy(out=Lend[:, 0:nch], in_=LTv[:, :, endcol]), reads=[LT], writes=[Lend])
            P.op(DVE, lambda LTv=LTv, Lend=Lend, Dsb=Dsb: nc.vector.tensor_tensor(out=Dsb[:, 0:T].rearrange("p (c j) -> p c j", j=64), in0=LTv,
                                                             in1=Lend[:, 0:nch].unsqueeze(2).to_broadcast([128, nch, 64]), op=ALU.subtract),
                 reads=[LT, Lend], writes=[Dsb])
            P.op(ACT, lambda EK=EK, Dsb=Dsb: nc.scalar.activation(out=EK[:, 0:T], in_=Dsb[:, 0:T], func=AF.Exp), reads=[Dsb], writes=[EK])
            P.op(ACT, lambda EQ=EQ, Dsb=Dsb: nc.scalar.activation(out=EQ[:, 0:T], in_=Dsb[:, 0:T], func=AF.Exp, scale=-1.0), reads=[Dsb], writes=[EQ])
            P.op(ACT, lambda h=h, Lend=Lend: nc.scalar.activation(out=edec[:, h, 0:nch], in_=Lend[:, 0:nch], func=AF.Exp, scale=-1.0),
                 reads=[Lend], writes=[edec])
            P.op(DVE, lambda h=h, EK=EK: nc.vector.tensor_tensor(out=khT[:, h, 0:T], in0=kT_sb[:, h, 0:T], in1=EK[:, 0:T], op=ALU.mult),
                 reads=[kT_sb, EK], writes=[khT])
            P.op(DVE, lambda h=h, EQ=EQ: nc.vector.tensor_tensor(out=qhT[:, h, 0:T], in0=qT_sb[:, h, 0:T], in1=EQ[:, 0:T], op=ALU.mult),
                 reads=[qT_sb, EQ], writes=[qhT])

    def gla_scan_tile(self, d_, tt, khT, qhT, edec, v_sb, sc, want_o):
        nc, P = self.nc, self.P
        bank = self.bank
        po = (bank[0], bank[1]); pP = (bank[2], bank[3]); psc = bank[4]; pT = self.pTb
        ktoks, PTs, Sbfs = sc
        ktok = ktoks[tt % 2]; PT = PTs[tt % 2]
        S = self.S[d_]
        c0, c1 = tt * 128, (tt + 1) * 128
        for h in range(4):
            P.op(PE, lambda h=h: nc.tensor.transpose(out=pT[:, h * 128:(h + 1) * 128], in_=khT[:, h, c0:c1], identity=self.identb[:]),
                 reads=[khT, self.identb], writes=[pT])
        P.op(ACT, lambda: nc.scalar.copy(out=ktok[:].rearrange("p h d -> p (h d)"), in_=pT[:, 0:512]), reads=[pT], writes=[ktok])
        if want_o:
            for h in range(4):
                P.op(PE, lambda h=h: nc.tensor.matmul(psc[:, h * 128:(h + 1) * 128], lhsT=khT[:, h, c0:c1], rhs=qhT[:, h, c0:c1],
                                                      start=True, stop=True), reads=[khT, qhT], writes=[psc])
            P.op(DVE, lambda: nc.vector.tensor_tensor(out=PT[:], in0=psc[:].rearrange("p (h t) -> p h t", h=4),
                                                     in1=self.maskfb[:, d_, :].unsqueeze(1).to_broadcast([128, 4, 128]), op=ALU.mult),
                 reads=[psc, self.maskfb], writes=[PT])
            for h in range(4):
                pb = po[h // 2]
                oc = (h % 2) * 256
                P.op(PE, lambda h=h, pb=pb, oc=oc: nc.tensor.matmul(pb[:, oc:oc + 256], lhsT=PT[:, h, :], rhs=v_sb[:, tt, h * 256:(h + 1) * 256],
                                                                   start=(h % 2 == 0), stop=False), reads=[PT, v_sb], writes=[pb])
        order = (0, 1) if d_ == 0 else (1, 0)
        for ci, ch in enumerate(order):
            r0, r1 = ch * 64, (ch + 1) * 64
            gch = tt * 2 + ch
            for h in range(4):
                pb = pP[h // 2]
                oc = (h % 2) * 256
                P.op(PE, lambda h=h, pb=pb, oc=oc, r0=r0, r1=r1: nc.tensor.matmul(pb[:, oc:oc + 256], lhsT=ktok[r0:r1, h, :],
                                                                   rhs=v_sb[r0:r1, tt, h * 256:(h + 1) * 256], start=True, stop=True),
                     reads=[ktok, v_sb], writes=[pb])
            if want_o:
                for h in range(4):
                    Sh = S[h]; Sb = Sbfs[ci][h]
                    P.op(ACT, lambda h=h, Sh=Sh, Sb=Sb, gch=gch: nc.scalar.activation(out=Sb[:], in_=Sh[:], func=AF.Identity, scale=edec[:, h, gch:gch + 1]),
                         reads=[Sh, edec], writes=[Sb])
                for h in range(4):
                    Sb = Sbfs[ci][h]
                    pb = po[h // 2]
                    oc = (h % 2) * 256
                    P.op(PE, lambda h=h, pb=pb, oc=oc, r0=r0, r1=r1, ci=ci, Sb=Sb: nc.tensor.matmul(pb[r0:r1, oc:oc + 256], lhsT=qhT[:, h, c0 + r0:c0 + r1], rhs=Sb[:],
                                                                       start=False, stop=(ci == 1)), reads=[qhT, Sb], writes=[pb])
            for h in range(4):
                Sh = S[h]
                pb2 = pP[h // 2]
                oc2 = (h % 2) * 256
                P.op(DVE, lambda h=h, Sh=Sh, pb2=pb2, oc2=oc2, gch=gch: nc.vector.scalar_tensor_tensor(
                    out=Sh[:], in0=Sh[:], scalar=edec[:, h, gch:gch + 1], in1=pb2[:, oc2:oc2 + 256], op0=ALU.mult, op1=ALU.add),
                    reads=[Sh, edec, pb2], writes=[Sh])
        return po

    def gla_seq(self, st0, l, j, x_src, x_dst, N, s, T, zero_state, want_out):
        nc, P = self.nc, self.P
        w = self.gw
        bank = self.bank
        gs = self.gscr
        nt_c = T // 128
        nch = T // 64
        nsteps = N // T
        pT = self.pTb
        with contextlib.ExitStack() as st:
            AT, BT, G = self.mod_prep(st, l, s, "mix")
            pn = self.prenorm_alloc(st, nt_c)
            hT = pn["hT"]
            qT_sb = self.sb(st, "qT_sb", [128, 4, T], F32)
            kT_sb = self.sb(st, "kT_sb", [128, 4, T], F32)
            v_sb = self.sb(st, "v_sb", [128, nt_c, D], BF16)
            uT = self.sb(st, "uT", [32, T], BF16)
            lf = self.sb(st, "lf", [128, nt_c, 512], F32)
            lbt = [self.sb(st, "lbt%d" % i, [128, 512], F32) for i in range(2)]
            zt = self.sb(st, "zt", [128, D], F32)
            khT = self.sb(st, "khT", [128, 4, T], BF16)
            qhT = self.sb(st, "qhT", [128, 4, T], BF16)
            edec = self.sb(st, "edec", [128, 4, nch], F32)
            Dsb = [self.sb(st, "Dsb%d" % i, [128, T], F32) for i in range(2)]
            EK = [self.sb(st, "EK%d" % i, [128, T], F32) for i in range(2)]
            EQ = [self.sb(st, "EQ%d" % i, [128, T], F32) for i in range(2)]
            Lend = [self.sb(st, "Lend%d" % i, [128, nch], F32) for i in range(2)]
            ktok = [self.sb(st, "ktok%d" % i, [128, 4, 128], BF16) for i in range(2)]
            PT = [self.sb(st, "PT%d" % i, [128, 4, 128], BF16) for i in range(2)]
            Sbf = [[self.sb(st, "Sbf%d_%d" % (i, h), [128, 256], BF16) for h in range(4)] for i in range(2)]
            of_sb = [self.sb(st, "of_sb%d" % i, [128, D], F32) for i in range(2)]
            if zero_state:
                for h in range(4):
                    P.op(POOL, lambda h=h: nc.gpsimd.memset(self.S[0][h][:], 0.0), writes=[self.S[0][h]])
            for t in range(nsteps):
                t0 = t * T
                self.prenorm_many(pn, x_src, [(t * nt_c + pos, pos) for pos in range(nt_c)], AT, BT, pT)
                for which, wn, dst, scl in (("q", "wq", qT_sb, 128.0 ** -0.5), ("k", "wk", kT_sb, 1.0)):
                    for h in range(4):
                        pb = bank[h % 2]
                        for k in range(8):
                            P.op(PE, lambda k=k, h=h, pb=pb, wn=wn: nc.tensor.matmul(pb[:, 0:T], lhsT=w[wn][:, k, h * 128:(h + 1) * 128], rhs=hT[:, k, 0:T],
                                                                                   start=(k == 0), stop=(k == 7)), reads=[w[wn], hT], writes=[pb])
                        P.op(ACT, lambda h=h, pb=pb, dst=dst, scl=scl: nc.scalar.mul(out=dst[:, h, :], in_=pb[:, 0:T], mul=scl), reads=[pb], writes=[dst])
                        P.dma(SP, gs[which + "T"][h, :, t0:t0 + T], dst[:, h, :], reads=[dst])
                pu = bank[2]
                for k in range(8):
                    P.op(PE, lambda k=k: nc.tensor.matmul(pu[0:32, 0:T], lhsT=w["wg1"][:, k, :], rhs=hT[:, k, 0:T], start=(k == 0), stop=(k == 7)),
                         reads=[w["wg1"], hT], writes=[pu])
                P.op(ACT, lambda: nc.scalar.copy(out=uT[:, 0:T], in_=pu[0:32, 0:T]), reads=[pu], writes=[uT])
                for tt in range(nt_c):
                    pv = (bank[3], bank[4])
                    for hf in range(2):
                        for k in range(8):
                            P.op(PE, lambda k=k, hf=hf, tt=tt: nc.tensor.matmul(pv[hf][:], lhsT=hT[:, k, tt * 128:(tt + 1) * 128], rhs=w["wv"][:, k, hf * 512:(hf + 1) * 512],
                                                                               start=(k == 0), stop=(k == 7)), reads=[hT, w["wv"]], writes=[pv[hf]])
                        P.op(DVE, lambda hf=hf, tt=tt: nc.vector.tensor_copy(out=v_sb[:, tt, hf * 512:(hf + 1) * 512], in_=pv[hf][:]), reads=[pv[hf]], writes=[v_sb])
                    P.dma(SP, gs["v"][t0 + tt * 128:t0 + (tt + 1) * 128, :], v_sb[:, tt, :], reads=[v_sb])
                    pz = (bank[5], bank[6])
                    for hf in range(2):
                        P.op(PE, lambda hf=hf, tt=tt: nc.tensor.matmul(pz[hf][:], lhsT=uT[:, tt * 128:(tt + 1) * 128], rhs=w["wg2"][:, hf * 512:(hf + 1) * 512],
                                                                      start=True, stop=True), reads=[uT, w["wg2"]], writes=[pz[hf]])
                        P.op(DVE, lambda hf=hf: nc.vector.tensor_tensor(out=zt[:, hf * 512:(hf + 1) * 512], in0=pz[hf][:], in1=w["bg"][:, hf * 512:(hf + 1) * 512], op=ALU.add),
                             reads=[pz[hf], w["bg"]], writes=[zt])
                    P.op(ACT, lambda: nc.scalar.activation(out=zt[:], in_=zt[:], func=AF.Exp, scale=-1.0), reads=[zt], writes=[zt])
                    lb_ = lbt[tt % 2]
                    P.op(ACT, lambda tt=tt: nc.scalar.activation(out=lf[:, tt, :], in_=zt[:, 0:512], func=AF.Ln, bias=1.0, scale=1.0), reads=[zt], writes=[lf])
                    P.op(ACT, lambda lb_=lb_: nc.scalar.activation(out=lb_[:], in_=zt[:, 512:1024], func=AF.Ln, bias=1.0, scale=1.0), reads=[zt], writes=[lb_])
                    P.dma(SP, gs["lb"][t0 + tt * 128:t0 + (tt + 1) * 128, :], lb_[:], reads=[lb_])
                self.gla_prep(st, 0, T, lf, qT_sb, kT_sb, (khT, qhT, edec, Dsb, EK, EQ, Lend))
                for tt in range(nt_c):
                    po = self.gla_scan_tile(0, tt, khT, qhT, edec, v_sb, (ktok, PT, Sbf), want_out)
                    if want_out:
                        ob = of_sb[tt % 2]
                        for hf in range(2):
                            P.op(ACT, lambda hf=hf, ob=ob: nc.scalar.copy(out=ob[:, hf * 512:(hf + 1) * 512], in_=po[hf][:]), reads=[po[hf]], writes=[ob])
                        P.dma(SP, gs["of"][t0 + tt * 128:t0 + (tt + 1) * 128, :], ob[:], reads=[ob])
            P.flush()
        with contextlib.ExitStack() as st:
            if want_out:
                po_ = self.post_alloc(st)
                AT, BT, G = self.mod_prep(st, l, s, "mix", tmp=po_["y"][0])
                pn = self.prenorm_alloc(st, nt_c)
                hT = pn["hT"]
            qT_sb = self.sb(st, "qT_sb", [128, 4, T], F32)
            kT_sb = self.sb(st, "kT_sb", [128, 4, T], F32)
            v_sb = self.sb(st, "v_sb", [128, nt_c, D], BF16)
            lb = self.sb(st, "lb", [128, nt_c, 512], F32)
            khT = self.sb(st, "khT", [128, 4, T], BF16)
            qhT = self.sb(st, "qhT", [128, 4, T], BF16)
            edec = self.sb(st, "edec", [128, 4, nch], F32)
            Dsb = [self.sb(st, "Dsb%d" % i, [128, T], F32) for i in range(2)]
            EK = [self.sb(st, "EK%d" % i, [128, T], F32) for i in range(2)]
            EQ = [self.sb(st, "EQ%d" % i, [128, T], F32) for i in range(2)]
            Lend = [self.sb(st, "Lend%d" % i, [128, nch], F32) for i in range(2)]
            ktok = [self.sb(st, "ktok%d" % i, [128, 4, 128], BF16) for i in range(2)]
            PT = [self.sb(st, "PT%d" % i, [128, 4, 128], BF16) for i in range(2)]
            Sbf = [[self.sb(st, "Sbf%d_%d" % (i, h), [128, 256], BF16) for h in range(4)] for i in range(2)]
            if want_out:
                of_sb = [self.sb(st, "of_sb%d" % i, [128, D], F32) for i in range(2)]
                o_sb = self.sb(st, "o_sb", [128, D], F32)
                rs_sb = self.sb(st, "rs_sb", [128, D], F32)
                on_b = self.sb(st, "on_b", [128, D], BF16)
                onT = self.sb(st, "onT", [128, 8, 128], BF16)
                ssh = self.sb(st, "ssh", [128, 4], F32)
                rsh = self.sb(st, "rsh", [128, 4], F32)
                h1 = self.sb(st, "h1", [128, 4], F32)
                h2 = self.sb(st, "h2", [128, 4], F32)
                h3 = self.sb(st, "h3", [128, 4], F32)
            if zero_state:
                for h in range(4):
                    P.op(POOL, lambda h=h: nc.gpsimd.memset(self.S[1][h][:], 0.0), writes=[self.S[1][h]])
            def load_qkl(t):
                t0_ = t * T
                P.dma(SP, qT_sb[:], gs["qT"][:, :, t0_:t0_ + T].rearrange("h p t -> p h t"), writes=[qT_sb])
                P.dma(SP, kT_sb[:], gs["kT"][:, :, t0_:t0_ + T].rearrange("h p t -> p h t"), writes=[kT_sb])
                P.dma(SP, lb[:], gs["lb"][t0_:t0_ + T, :].rearrange("(tt p) c -> p tt c", p=128), writes=[lb])

            load_qkl(nsteps - 1)
            for t in reversed(range(nsteps)):
                t0 = t * T
                P.dma(SP, v_sb[:], gs["v"][t0:t0 + T, :].rearrange("(tt p) c -> p tt c", p=128), writes=[v_sb])
                if want_out:
                    self.prenorm_many(pn, x_src, [(t * nt_c + pos, pos) for pos in range(nt_c)], AT, BT, pT)
                self.gla_prep(st, 1, T, lb, qT_sb, kT_sb, (khT, qhT, edec, Dsb, EK, EQ, Lend))
                if t > 0:
                    load_qkl(t - 1)
                for tt in reversed(range(nt_c)):
                    jt = t * nt_c + tt
                    if want_out:
                        ob = of_sb[tt % 2]
                        P.dma(SP, ob[:], gs["of"][t0 + tt * 128:t0 + (tt + 1) * 128, :], writes=[ob])
                    if want_out:
                        pr = (bank[5], bank[6])
                        for hf in range(2):
                            for k in range(8):
                                P.op(PE, lambda k=k, hf=hf, tt=tt: nc.tensor.matmul(pr[hf][:], lhsT=hT[:, k, tt * 128:(tt + 1) * 128], rhs=w["wr"][:, k, hf * 512:(hf + 1) * 512],
                                                                                           start=(k == 0), stop=(k == 7)), reads=[hT, w["wr"]], writes=[pr[hf]])
                            P.op(ACT, lambda hf=hf: nc.scalar.activation(out=rs_sb[:, hf * 512:(hf + 1) * 512], in_=pr[hf][:], func=AF.Silu), reads=[pr[hf]], writes=[rs_sb])
                        P.op(DVE, lambda: nc.vector.tensor_tensor(out=rs_sb[:], in0=rs_sb[:], in1=w["ng"][:], op=ALU.mult), reads=[rs_sb, w["ng"]], writes=[rs_sb])
                    po = self.gla_scan_tile(1, tt, khT, qhT, edec, v_sb, (ktok, PT, Sbf), want_out)
                    if not want_out:
                        continue
                    for hf in range(2):
                        P.op(DVE, lambda hf=hf, ob=ob: nc.vector.tensor_tensor(out=o_sb[:, hf * 512:(hf + 1) * 512], in0=po[hf][:], in1=ob[:, hf * 512:(hf + 1) * 512], op=ALU.add),
                             reads=[po[hf], ob], writes=[o_sb])
                    for h in range(4):
                        P.op(ACT, lambda h=h: nc.scalar.activation(out=self.junk[:, h * 256:(h + 1) * 256], in_=o_sb[:, h * 256:(h + 1) * 256], func=AF.Square,
                                                                   accum_out=ssh[:, h:h + 1]), reads=[o_sb], writes=[self.junk, ssh])
                    self.rsqrt_pool(rsh, ssh, 4, 1.0 / 256.0)
                    for h in range(4):
                        P.op(DVE, lambda h=h: nc.vector.scalar_tensor_tensor(out=on_b[:, h * 256:(h + 1) * 256], in0=o_sb[:, h * 256:(h + 1) * 256], scalar=rsh[:, h:h + 1],
                                                                             in1=rs_sb[:, h * 256:(h + 1) * 256], op0=ALU.mult, op1=ALU.mult),
                             reads=[o_sb, rsh, rs_sb], writes=[on_b])
                    for k in range(8):
                        P.op(PE, lambda k=k: nc.tensor.transpose(out=pT[:, k * 128:(k + 1) * 128], in_=on_b[:, k * 128:(k + 1) * 128], identity=self.identb[:]),
                             reads=[on_b, self.identb], writes=[pT])
                    P.op(ACT, lambda: nc.scalar.copy(out=onT[:].rearrange("p k t -> p (k t)"), in_=pT[:]), reads=[pT], writes=[onT])
                    py = (bank[5], bank[6])
                    for hf in range(2):
                        for k in range(8):
                            P.op(PE, lambda k=k, hf=hf: nc.tensor.matmul(py[hf][:], lhsT=onT[:, k, :], rhs=w["wo"][:, k, hf * 512:(hf + 1) * 512],
                                                                        start=(k == 0), stop=(k == 7)), reads=[onT, w["wo"]], writes=[py[hf]])
                    self.post_tile(po_, py, x_src, x_dst, jt, G)
            P.flush()

def _consts():
    c = np.zeros((128, 6, 128), np.float32)
    i = np.arange(128)
    c[:, 0, :] = np.eye(128)
    c[:, 1, :] = 1.0 / 1024.0
    same = (i[:, None] // 64) == (i[None, :] // 64)
    c[:, 2, :] = ((i[:, None] <= i[None, :]) & same) / 16.0
    c[:, 3, :] = ((i[:, None] >= i[None, :]) & same) / 16.0
    c[:, 4, :] = ((i[:, None] <= i[None, :]) & same) * 1.0
    c[:, 5, :] = ((i[:, None] >= i[None, :]) & same) * 1.0
    return c


def _fm(v, nk):
    v = np.asarray(v)
    lead = v.shape[:-1]
    return np.ascontiguousarray(np.swapaxes(v.reshape(lead + (nk, 128)), -1, -2))


def prep_inputs(inp):
    f = lambda a: np.ascontiguousarray(np.asarray(a, dtype=np.float32))
    shared = {}
    shared["ada_w"] = f(inp["ada_w"])
    shared["ada_b"] = f(inp["ada_b"])
    for k in ("norm_pre_mix", "norm_post_mix", "norm_pre_ffn", "norm_post_ffn"):
        shared[k] = f(inp[k])
    shared["npre_mixT"] = f(_fm(inp["norm_pre_mix"], 8))
    shared["npre_ffnT"] = f(_fm(inp["norm_pre_ffn"], 8))
    shared["cf_w1"] = f(inp["cf_w1"])
    shared["cf_b1T"] = f(_fm(inp["cf_b1"], 16))
    shared["cf_dwT"] = f(np.transpose(np.asarray(inp["cf_dw"]).reshape(2, KW, 8, 128), (0, 3, 2, 1)))
    shared["cf_dwbT"] = f(_fm(inp["cf_dwb"], 8))
    shared["cf_lngT"] = f(_fm(inp["cf_ln_g"], 8))
    shared["cf_lnbT"] = f(_fm(inp["cf_ln_b"], 8))
    shared["cf_w2"] = f(inp["cf_w2"])
    shared["cf_b2"] = f(inp["cf_b2"])
    shared["gla_wq"] = f(inp["gla_wq"])
    shared["gla_wk"] = f(inp["gla_wk"])
    shared["gla_wv"] = f(inp["gla_wv"])
    shared["gla_wr"] = f(inp["gla_wr"])
    wg1 = np.asarray(inp["gla_wg1"])
    shared["gla_wg1c"] = f(np.concatenate([wg1[:, 0], wg1[:, 1]], axis=-1))
    wg2 = np.asarray(inp["gla_wg2"])
    blk = np.zeros((2, 32, 1024), np.float32)
    blk[:, 0:16, 0:512] = wg2[:, 0]
    blk[:, 16:32, 512:1024] = wg2[:, 1]
    shared["gla_wg2c"] = blk
    bg = np.asarray(inp["gla_bg"])
    shared["gla_bgc"] = f(bg.reshape(2, 1024))
    ng = np.asarray(inp["gla_norm_g"])
    shared["gla_ngc"] = f(np.tile(ng, (1, 4)))
    shared["gla_wo"] = f(inp["gla_wo"])
    shared["ffn_wa"] = f(inp["ffn_wa"])
    shared["ffn_wb"] = f(inp["ffn_wb"])
    shared["ffn_dwT"] = f(np.transpose(np.asarray(inp["ffn_dw"]).reshape(DEPTH, 9, NFC, 128), (0, 3, 2, 1)))
    shared["ffn_dwbT"] = f(_fm(inp["ffn_dwb"], NFC))
    shared["ffn_wo"] = f(inp["ffn_wo"])
    shared["consts"] = _consts()
    x = np.asarray(inp["x"], dtype=np.float32)
    c = np.asarray(inp["c"], dtype=np.float32)
    ctx = np.asarray(inp["ctx"], dtype=np.float32)
    c_ctx = np.asarray(inp["c_ctx"], dtype=np.float32)
    maps = []
    for b in range(x.shape[0]):
        m = dict(shared)
        m["x"] = np.ascontiguousarray(x[b])
        m["ctx"] = np.ascontiguousarray(ctx[b])
        cT = np.stack([c[b].reshape(8, 128).T, c_ctx.reshape(8, 128).T], axis=-1)
        m["cT"] = np.ascontiguousarray(cT)
        maps.append(m)
    return maps


_NC_CACHE = {}


def kernel(**inputs):
    maps = prep_inputs(inputs)
    if "nc" not in _NC_CACHE:
        _NC_CACHE["nc"] = KB().build()
    nc = _NC_CACHE["nc"]
    res = run_bass_kernel_spmd(nc, maps, core_ids=list(range(NCORES)))
    return np.stack([np.asarray(r["out"], dtype=np.float32) for r in res.results], axis=0)
```

```python
import contextlib
import numpy as np
import concourse.bass as bass
import concourse.mybir as mybir
from concourse.bass_utils import run_bass_kernel_spmd

F32 = mybir.dt.float32
BF16 = mybir.dt.bfloat16
I32 = mybir.dt.int32
ALU = mybir.AluOpType
AF = mybir.ActivationFunctionType

PE, ACT, DVE, POOL, SP = "pe", "act", "dve", "pool", "sp"
COMPUTE = (PE, ACT, DVE, POOL)

D = 1024
NSEQ = 4096
NCTX = 256
DEPTH = 4
FF = 2816
NFC = FF // 128
KW = 31
EPS = 1e-6
NCORES = 8


class Buf:
    __slots__ = ("name", "t", "lastw", "readers")

    def __init__(self, name, t=None):
        self.name = name
        self.t = t
        self.lastw = None
        self.readers = []

    def __getitem__(self, idx):
        return self.t[idx]


class Instr:
    __slots__ = ("eng", "fn", "reads", "writes", "deps", "signal", "is_dma", "sem", "val")

    def __init__(self, eng, fn, reads, writes, is_dma):
        self.eng = eng
        self.fn = fn
        self.reads = reads
        self.writes = writes
        self.deps = []
        self.signal = False
        self.is_dma = is_dma
        self.sem = None
        self.val = None


class Prog:
    def __init__(self, nc, st, n_dma_sems=64):
        self.nc = nc
        self.instrs = []
        self.bufs = []
        self.n_dma_sems = n_dma_sems
        self.eng = {PE: nc.tensor, ACT: nc.scalar, DVE: nc.vector, POOL: nc.gpsimd, SP: nc.sync}
        self.esem = {e: st.enter_context(nc.semaphore("s_" + e)) for e in COMPUTE}
        self.dsems = [st.enter_context(nc.semaphore("d%d" % i)) for i in range(n_dma_sems)]
        self.ecount = {e: 0 for e in COMPUTE}
        self.dcount = [0] * n_dma_sems
        self.ndma = 0
        self.waited = {}
        self.total = 0

    def buf(self, name, t=None):
        b = Buf(name, t)
        self.bufs.append(b)
        return b

    def op(self, eng, fn, reads=(), writes=()):
        self._add(Instr(eng, fn, tuple(reads), tuple(writes), False))

    def dma(self, q, out_ap, in_ap, reads=(), writes=(), **kw):
        eng = self.eng[q]
        fn = (lambda eng=eng, o=out_ap, i=in_ap, kw=kw: eng.dma_start(out=o, in_=i, **kw))
        self._add(Instr(q, fn, tuple(reads), tuple(writes), True))

    def _add(self, ins):
        idx = len(self.instrs)
        deps = set()
        for b in ins.reads:
            if b.lastw is not None:
                deps.add(b.lastw)
        for b in ins.writes:
            if b.lastw is not None:
                deps.add(b.lastw)
            deps.update(b.readers)
        deps.discard(idx)
        keep = []
        for d in deps:
            p = self.instrs[d]
            if p.is_dma or ins.is_dma:
                keep.append(d)
                continue
            if p.eng == ins.eng:
                if ins.eng == PE:
                    continue
                raw = any((b.lastw == d) for b in ins.reads)
                if not raw:
                    continue
            keep.append(d)
        ins.deps = keep
        for b in ins.reads:
            b.readers.append(idx)
        for b in ins.writes:
            b.lastw = idx
            b.readers = []
        self.instrs.append(ins)

    def flush(self, final=False):
        instrs = self.instrs
        for ins in instrs:
            for d in ins.deps:
                instrs[d].signal = True
        last = {}
        for i, ins in enumerate(instrs):
            if not ins.is_dma:
                last[ins.eng] = i
        for e, i in last.items():
            instrs[i].signal = True
        nds = self.n_dma_sems
        for ins in instrs:
            e = self.eng[ins.eng]
            need = {}
            for d in ins.deps:
                p = instrs[d]
                key = id(p.sem)
                if key not in need or need[key][1] < p.val:
                    need[key] = (p.sem, p.val)
            if ins.is_dma:
                k = self.ndma % nds
                self.ndma += 1
                if self.dcount[k] > 0:
                    key = id(self.dsems[k])
                    v = self.dcount[k] * 16
                    if key not in need or need[key][1] < v:
                        need[key] = (self.dsems[k], v)
            for key, (sem, v) in need.items():
                wk = (ins.eng, key)
                if self.waited.get(wk, -1) >= v:
                    continue
                self.waited[wk] = v
                e.wait_ge(sem, v)
            bi = ins.fn()
            if ins.is_dma:
                self.dcount[k] += 1
                ins.sem = self.dsems[k]
                ins.val = self.dcount[k] * 16
                bi.then_inc(self.dsems[k], 16)
            elif ins.signal:
                self.ecount[ins.eng] += 1
                ins.sem = self.esem[ins.eng]
                ins.val = self.ecount[ins.eng]
                bi.then_inc(self.esem[ins.eng], 1)
        engs = [SP] if final else [PE, ACT, DVE, POOL, SP]
        for en in engs:
            e = self.eng[en]
            for k in range(nds):
                if self.dcount[k] > 0:
                    key = (en, id(self.dsems[k]))
                    v = self.dcount[k] * 16
                    if self.waited.get(key, -1) < v:
                        self.waited[key] = v
                        e.wait_ge(self.dsems[k], v)
            for src in COMPUTE:
                v = self.ecount[src]
                if v > 0:
                    key = (en, id(self.esem[src]))
                    if self.waited.get(key, -1) < v:
                        self.waited[key] = v
                        e.wait_ge(self.esem[src], v)
        self.total += len(instrs)
        self.instrs = []
        for b in self.bufs:
            b.lastw = None
            b.readers = []


class KB:
    def __init__(self, n_stage=99, debug=False):
        self.n_stage = n_stage
        self.debug = debug
        self.nc = bass.Bass("TRN2", target_bir_lowering=False)
        self.gst = contextlib.ExitStack()
        self.P = Prog(self.nc, self.gst)
        self.din = {}
        self.uid = 0
        self.dbg_scan = False

    def sb(self, st, name, shape, dt):
        self.uid += 1
        return self.P.buf(name, st.enter_context(self.nc.sbuf_tensor("%s_%d" % (name, self.uid), shape, dt)))

    def ps(self, st, name, shape, dt):
        self.uid += 1
        return self.P.buf(name, st.enter_context(self.nc.psum_tensor("%s_%d" % (name, self.uid), shape, dt)))

    def dram_in(self, name, shape):
        t = self.nc.dram_tensor(name, list(shape), F32, kind="ExternalInput").ap()
        self.din[name] = t
        return t

    def dbg_dump(self, name, buf, shape, dt, ap=None):
        if not self.debug:
            return
        t = self.nc.dram_tensor(name, list(shape), dt, kind="ExternalOutput").ap()
        self.P.dma(SP, t, ap if ap is not None else buf[:], reads=[buf])

    def rsqrt(self, out_ap, x_ap, xbuf, obuf, t1, t2, yv, shape_sl, iters=3):
        nc, P = self.nc, self.P
        sl = shape_sl
        P.op(DVE, lambda: nc.vector.tensor_scalar(out=yv[sl].bitcast(I32), in0=x_ap.bitcast(I32), scalar1=1, scalar2=None,
                                                  op0=ALU.arith_shift_right), reads=[xbuf], writes=[yv])
        P.op(DVE, lambda: nc.vector.tensor_scalar(out=yv[sl].bitcast(I32), in0=yv[sl].bitcast(I32), scalar1=-1,
                                                  scalar2=0x5f3759df, op0=ALU.mult, op1=ALU.add), reads=[yv], writes=[yv])
        for it in range(iters):
            P.op(DVE, lambda: nc.vector.tensor_tensor(out=t1[sl], in0=yv[sl], in1=yv[sl], op=ALU.mult), reads=[yv], writes=[t1])
            P.op(DVE, lambda: nc.vector.scalar_tensor_tensor(out=t2[sl], in0=t1[sl], scalar=-0.5, in1=x_ap, op0=ALU.mult,
                                                             op1=ALU.mult), reads=[t1, xbuf], writes=[t2])
            last = it == iters - 1
            dst = out_ap if last else yv[sl]
            P.op(DVE, lambda dst=dst: nc.vector.scalar_tensor_tensor(out=dst, in0=t2[sl], scalar=1.5, in1=yv[sl], op0=ALU.add,
                                                                     op1=ALU.mult), reads=[t2, yv], writes=[obuf if last else yv])

    def rsqrt_pool(self, rstd, ss, n, scale):
        nc, P = self.nc, self.P
        P.op(POOL, lambda: nc.gpsimd.tensor_scalar(out=ss[:, 0:n], in0=ss[:, 0:n], scalar1=scale, scalar2=EPS, op0=ALU.mult, op1=ALU.add),
             reads=[ss], writes=[ss])
        P.op(POOL, lambda: nc.gpsimd.tensor_tensor(out=rstd[:, 0:n], in0=ss[:, 0:n], in1=self.neghalf[:, 0:n], op=ALU.pow),
             reads=[ss, self.neghalf], writes=[rstd])

    def load_w_bf16(self, st_stage, dst, dst_ap_fn, src_ap, nk, ncols, chunk=512, q=SP):
        nc, P = self.nc, self.P
        stg = self.stg
        srcv = src_ap.rearrange("(k p) n -> p k n", p=128)
        i = 0
        for k0 in range(0, nk, 8):
            k1 = min(nk, k0 + 8)
            for c0 in range(0, ncols, chunk):
                c1 = min(ncols, c0 + chunk)
                sg = stg[self.stg_i % len(stg)]
                self.stg_i += 1
                P.dma(q, sg[:, 0:k1 - k0, 0:c1 - c0], srcv[:, k0:k1, c0:c1], writes=[sg])
                eng = (ACT, DVE)[i % 2]
                i += 1
                o = dst_ap_fn(k0, k1, c0, c1)
                src = sg[:, 0:k1 - k0, 0:c1 - c0]
                if eng == POOL:
                    P.op(POOL, lambda o=o, src=src: nc.gpsimd.tensor_copy(out=o, in_=src), reads=[sg], writes=[dst])
                elif eng == ACT:
                    P.op(ACT, lambda o=o, src=src: nc.scalar.copy(out=o, in_=src), reads=[sg], writes=[dst])
                else:
                    P.op(DVE, lambda o=o, src=src: nc.vector.tensor_copy(out=o, in_=src), reads=[sg], writes=[dst])

    def bc_load(self, dst, src1d, q=SP):
        self.P.dma(q, dst[:], src1d.partition_broadcast(128), writes=[dst])

    def build(self):
        nc, P = self.nc, self.P
        gst = self.gst
        with gst:
            di = self.dram_in
            x_in = di("x", [NSEQ, D])
            ctx_in = di("ctx", [NCTX, D])
            cT = di("cT", [128, 8, 2])
            ada_w = di("ada_w", [DEPTH, D, 6 * D])
            ada_b = di("ada_b", [DEPTH, 6 * D])
            self.norms = {k: di(k, [DEPTH, D]) for k in ("norm_pre_mix", "norm_post_mix", "norm_pre_ffn", "norm_post_ffn")}
            self.cf = dict(w1=di("cf_w1", [2, D, 2 * D]), b1T=di("cf_b1T", [2, 128, 16]), dwT=di("cf_dwT", [2, 128, 8, KW]),
                           dwbT=di("cf_dwbT", [2, 128, 8]), lngT=di("cf_lngT", [2, 128, 8]), lnbT=di("cf_lnbT", [2, 128, 8]),
                           w2=di("cf_w2", [2, D, D]), b2=di("cf_b2", [2, D]))
            self.gl = dict(wq=di("gla_wq", [2, D, 512]), wk=di("gla_wk", [2, D, 512]), wv=di("gla_wv", [2, D, D]),
                           wr=di("gla_wr", [2, D, D]), wg1=di("gla_wg1c", [2, D, 32]), wg2=di("gla_wg2c", [2, 32, 1024]),
                           bg=di("gla_bgc", [2, 1024]), ng=di("gla_ngc", [2, 1024]), wo=di("gla_wo", [2, D, D]))
            self.ff = dict(wa=di("ffn_wa", [DEPTH, D, FF]), wb=di("ffn_wb", [DEPTH, D, FF]), dwT=di("ffn_dwT", [DEPTH, 128, NFC, 9]),
                           dwbT=di("ffn_dwbT", [DEPTH, 128, NFC]), wo=di("ffn_wo", [DEPTH, FF, D]))
            consts = di("consts", [128, 6, 128])
            out = nc.dram_tensor("out", [NSEQ, D], F32, kind="ExternalOutput").ap()
            dbg_ctx = nc.dram_tensor("dbg_ctx", [NCTX, D], F32, kind="ExternalOutput").ap() if self.debug else None
            self.mod = nc.dram_tensor("mod", [DEPTH, 2, 6 * D], F32, kind="Internal").ap()
            self.cdiag = nc.dram_tensor("cdiag", [8, 128, KW, 128], BF16, kind="Internal").ap()
            self.fdiag = nc.dram_tensor("fdiag", [NFC, 128, 9, 128], BF16, kind="Internal").ap()
            self.modT = nc.dram_tensor("modT", [DEPTH, 2, 128, 48], F32, kind="Internal").ap()
            self.npreT = {"mix": di("npre_mixT", [DEPTH, 128, 8]), "ffn": di("npre_ffnT", [DEPTH, 128, 8])}
            xs = [nc.dram_tensor("xs%d" % i, [NSEQ, D], F32, kind="Internal").ap() for i in range(2)]
            cs = [nc.dram_tensor("cs%d" % i, [NCTX, D], F32, kind="Internal").ap() for i in range(2)]
            gk = "ExternalOutput" if self.debug else "Internal"
            self.gscr = dict(
                qT=nc.dram_tensor("g_qT", [4, 128, NSEQ], F32, kind=gk).ap(),
                kT=nc.dram_tensor("g_kT", [4, 128, NSEQ], F32, kind=gk).ap(),
                v=nc.dram_tensor("g_v", [NSEQ, D], BF16, kind=gk).ap(),
                lb=nc.dram_tensor("g_lb", [NSEQ, 512], F32, kind=gk).ap(),
                of=nc.dram_tensor("g_of", [NSEQ, D], F32, kind=gk).ap(),
            )

            self.consts = consts
            self.identb = self.sb(gst, "identb", [128, 128], BF16)
            self.onesb = self.sb(gst, "onesb", [128, 128], BF16)
            self.maskfb = self.sb(gst, "maskfb", [128, 2, 128], BF16)
            self.bank = [self.ps(gst, "bank%d" % i, [128, 512], F32) for i in range(7)]
            self.pTb = self.ps(gst, "pTb", [128, 1024], BF16)
            self.junk = self.sb(gst, "junk", [128, D], BF16)
            self.neghalf = self.sb(gst, "neghalf", [128, 8], F32)
            P.op(POOL, lambda: nc.gpsimd.memset(self.neghalf[:], -0.5), writes=[self.neghalf])
            with contextlib.ExitStack() as stc:
                cst = self.sb(stc, "cst", [128, 6, 128], F32)
                P.dma(SP, cst[:], consts, writes=[cst])
                P.op(POOL, lambda: nc.gpsimd.tensor_copy(out=self.identb[:], in_=cst[:, 0, :]), reads=[cst], writes=[self.identb])
                P.op(POOL, lambda: nc.gpsimd.tensor_copy(out=self.onesb[:], in_=cst[:, 1, :]), reads=[cst], writes=[self.onesb])
                P.op(POOL, lambda: nc.gpsimd.tensor_copy(out=self.maskfb[:], in_=cst[:, 4:6, :]), reads=[cst], writes=[self.maskfb])
                P.flush()

            self.adaln_phase(cT, ada_w, ada_b)
            P.flush()

            stage = 0
            xcur, ccur = x_in, ctx_in
            xi, ci = 0, 0
            done = False
            for l in range(DEPTH):
                last = l == DEPTH - 1
                j = l // 2
                if stage >= self.n_stage:
                    break
                if l % 2 == 0:
                    with contextlib.ExitStack() as st:
                        self.conf_load(st, j)
                        P.flush()
                        xo = xs[xi]; xi ^= 1
                        self.conf_pass(st, l, j, xcur, xo, NSEQ, 0, 512)
                        xcur = xo
                        if not last:
                            co = cs[ci]; ci ^= 1
                            self.conf_pass(st, l, j, ccur, co, NCTX, 1, 256)
                            ccur = co
                        P.flush()
                else:
                    with contextlib.ExitStack() as st:
                        self.gla_load(st, j)
                        P.flush()
                        co = None
                        if not last:
                            co = cs[ci]; ci ^= 1
                        self.gla_seq(st, l, j, ccur, co, NCTX, 1, 256, zero_state=True, want_out=not last)
                        if co is not None:
                            ccur = co
                        xo = xs[xi]; xi ^= 1
                        self.gla_seq(st, l, j, xcur, xo, NSEQ, 0, 512, zero_state=False, want_out=True)
                        xcur = xo
                        P.flush()
                stage += 1
                if stage >= self.n_stage:
                    break
                with contextlib.ExitStack() as st:
                    self.ffn_load(st, l)
                    P.flush()
                    xo = out if last else xs[xi]
                    xi ^= 1
                    self.ffn_pass(st, l, xcur, xo, NSEQ, 0, 512, True)
                    xcur = xo
                    if not last:
                        co = cs[ci]; ci ^= 1
                        self.ffn_pass(st, l, ccur, co, NCTX, 1, 256, False)
                        ccur = co
                    P.flush()
                stage += 1
            if xcur is not out:
                with contextlib.ExitStack() as st:
                    tb = [self.sb(st, "cp%d" % i, [128, D], F32) for i in range(2)]
                    for t in range(NSEQ // 128):
                        b = tb[t % 2]
                        P.dma(SP, b[:], xcur[t * 128:(t + 1) * 128, :], writes=[b])
                        P.dma(SP, out[t * 128:(t + 1) * 128, :], b[:], reads=[b])
                    P.flush()
            if self.debug:
                with contextlib.ExitStack() as st:
                    tb = [self.sb(st, "cq%d" % i, [128, D], F32) for i in range(2)]
                    for t in range(NCTX // 128):
                        b = tb[t % 2]
                        P.dma(SP, b[:], ccur[t * 128:(t + 1) * 128, :], writes=[b])
                        P.dma(SP, dbg_ctx[t * 128:(t + 1) * 128, :], b[:], reads=[b])
                    P.flush()
            P.flush(final=True)
        return nc

    def adaln_phase(self, cT, ada_w, ada_b):
        nc, P = self.nc, self.P
        with contextlib.ExitStack() as st:
            sc = self.sb(st, "sc", [128, 8, 2], F32)
            wts = [self.sb(st, "adaw%d" % i, [128, 8, 512], F32) for i in range(3)]
            msb = self.sb(st, "msb", [2, 6 * D], F32)
            bb = self.sb(st, "bb", [2, 6 * D], F32)
            mTs = [self.sb(st, "mT%d" % i, [128, 2, 48], F32) for i in range(2)]
            self.identf = self.sb(st, "identf", [128, 128], F32)
            P.dma(SP, self.identf[:], self.consts[:, 0, :], writes=[self.identf])
            P.dma(SP, sc[:], cT, writes=[sc])
            P.op(ACT, lambda: nc.scalar.activation(out=sc[:], in_=sc[:], func=AF.Silu), reads=[sc], writes=[sc])
            it = 0
            for l in range(DEPTH):
                P.dma(SP, bb[:], ada_b[l].partition_broadcast(2), writes=[bb])
                wv = ada_w[l].rearrange("(k p) n -> p k n", p=128)
                for n in range(12):
                    wt = wts[it % 3]
                    pm = self.bank[it % 2]
                    it += 1
                    P.dma(SP if n % 2 == 0 else ACT, wt[:], wv[:, :, n * 512:(n + 1) * 512], writes=[wt])
                    for k in range(8):
                        P.op(PE, lambda k=k, wt=wt, pm=pm: nc.tensor.matmul(pm[0:2, :], lhsT=sc[:, k, :], rhs=wt[:, k, :],
                                                                           start=(k == 0), stop=(k == 7)),
                             reads=[sc, wt], writes=[pm])
                    P.op(DVE, lambda n=n, pm=pm: nc.vector.tensor_tensor(out=msb[:, n * 512:(n + 1) * 512], in0=pm[0:2, :],
                                                                       in1=bb[:, n * 512:(n + 1) * 512], op=ALU.add),
                         reads=[pm, bb], writes=[msb])
                P.dma(SP, self.mod[l], msb[:], reads=[msb])
                pX = self.bank[2 + (l % 2)]
                for cidx in range(48):
                    P.op(PE, lambda cidx=cidx, pX=pX: nc.tensor.transpose(out=pX[:, 2 * cidx:2 * cidx + 2], in_=msb[0:2, cidx * 128:(cidx + 1) * 128],
                                                                         identity=self.identf[0:2, 0:2]), reads=[msb, self.identf], writes=[pX])
                mT = mTs[l % 2]
                P.op(ACT, lambda pX=pX, mT=mT: nc.scalar.copy(out=mT[:], in_=pX[:, 0:96].rearrange("p (c s) -> p s c", s=2)), reads=[pX], writes=[mT])
                for s_ in range(2):
                    P.dma(SP, self.modT[l, s_], mT[:, s_, :], reads=[mT])
            P.flush()

    def mod_prep(self, st, l, s, which, tmp=None):
        nc, P = self.nc, self.P
        base = 0 if which == "mix" else 3 * D
        cb = 0 if which == "mix" else 24
        npost = self.norms["norm_post_mix" if which == "mix" else "norm_post_ffn"][l]
        AT = self.sb(st, "AT", [128, 8], F32)
        BT = self.sb(st, "BT", [128, 8], F32)
        G = self.sb(st, "G", [128, D], F32)
        with contextlib.ExitStack() as st2:
            own = tmp is None
            if own:
                tmp = self.sb(st2, "mtmp", [128, D], F32)
            tmp8 = self.sb(st if not own else st2, "mtmp8", [128, 8], F32)
            m = self.mod[l, s]
            mT = self.modT[l, s]
            P.dma(SP, BT[:], mT[:, cb:cb + 8], writes=[BT])
            P.dma(SP, AT[:], mT[:, cb + 8:cb + 16], writes=[AT])
            P.dma(SP, tmp8[:], self.npreT[which][l], writes=[tmp8])
            P.op(DVE, lambda: nc.vector.scalar_tensor_tensor(out=AT[:], in0=AT[:], scalar=1.0, in1=tmp8[:], op0=ALU.add, op1=ALU.mult),
                 reads=[AT, tmp8], writes=[AT])
            self.bc_load(G, m[base + 2 * D:base + 3 * D])
            self.bc_load(tmp, npost)
            P.op(DVE, lambda: nc.vector.tensor_tensor(out=G[:], in0=G[:], in1=tmp[:], op=ALU.mult), reads=[G, tmp], writes=[G])
            if own:
                P.flush()
        return AT, BT, G

    def prenorm_alloc(self, st, ntile_h, nbuf=2):
        d = {}
        d["xt"] = [self.sb(st, "xt%d" % i, [128, D], F32) for i in range(nbuf)]
        d["hb"] = [self.sb(st, "hb%d" % i, [128, D], BF16) for i in range(nbuf)]
        d["ss"] = [self.sb(st, "ss%d" % i, [128, 1], F32) for i in range(2)]
        d["r1"] = self.sb(st, "r1", [128, 1], F32)
        d["r2"] = self.sb(st, "r2", [128, 1], F32)
        d["r3"] = self.sb(st, "r3", [128, 1], F32)
        d["rstd"] = [self.sb(st, "rstd%d" % i, [128, 1], F32) for i in range(2)]
        d["hT"] = self.sb(st, "hT", [128, 8, ntile_h if ntile_h > 64 else ntile_h * 128], BF16)
        d["i"] = 0
        d["li"] = 0
        return d

    def prenorm_tile(self, pn, x_src, j, pos, AT, BT, pT, cmap=None):
        tok = self.prenorm_A(pn, x_src, j, pos, cmap)
        self.prenorm_B(pn, tok, AT, BT, pT)

    def prenorm_many(self, pn, x_src, jlist, AT, BT, pT):
        n = len(jlist)
        lt, toks = {}, {}
        for i in range(min(2, n)):
            lt[i] = self.prenorm_load(pn, x_src, jlist[i][0])
        if n:
            toks[0] = self.prenorm_stat(pn, lt[0], jlist[0][1])
        for i in range(n):
            if i + 2 < n:
                lt[i + 2] = self.prenorm_load(pn, x_src, jlist[i + 2][0])
            if i + 1 < n:
                toks[i + 1] = self.prenorm_stat(pn, lt[i + 1], jlist[i + 1][1])
            self.prenorm_B(pn, toks[i], AT, BT, pT)

    def prenorm_load(self, pn, x_src, j, q=SP):
        if j is None:
            return None
        nb = len(pn["xt"])
        i = pn["li"]
        pn["li"] += 1
        xt = pn["xt"][i % nb]
        self.P.dma(q, xt[:], x_src[j * 128:(j + 1) * 128, :], writes=[xt])
        return i

    def prenorm_stat(self, pn, ltok, pos, cmap=None):
        nc, P = self.nc, self.P
        if cmap is None:
            cmap = (pos * 128, 0, 128)
        if ltok is None:
            return (None, cmap)
        nb = len(pn["xt"])
        xt = pn["xt"][ltok % nb]
        hb = pn["hb"][ltok % len(pn["hb"])]
        ss = pn["ss"][ltok % 2]
        rstd = pn["rstd"][ltok % 2]
        junk = self.junk
        P.op(ACT, lambda: nc.scalar.activation(out=junk[:], in_=xt[:], func=AF.Square, accum_out=ss[:]), reads=[xt], writes=[junk, ss])
        self.rsqrt_pool(rstd, ss, 1, 1.0 / D)
        P.op(DVE, lambda: nc.vector.tensor_scalar(out=hb[:], in0=xt[:], scalar1=rstd[:, 0:1], scalar2=None, op0=ALU.mult),
             reads=[xt, rstd], writes=[hb])
        return (hb, cmap)

    def prenorm_A(self, pn, x_src, j, pos, cmap=None, q=None):
        lt = self.prenorm_load(pn, x_src, j, q=q if q is not None else SP)
        return self.prenorm_stat(pn, lt, pos, cmap)

    def prenorm_B(self, pn, tok, AT, BT, pT):
        nc, P = self.nc, self.P
        hT = pn["hT"]
        hb, (dcol, slo, shi) = tok
        if hb is None:
            P.op(POOL, lambda: nc.gpsimd.memset(hT[:, :, dcol:dcol + (shi - slo)], 0.0), writes=[hT])
            return
        for k in range(8):
            P.op(PE, lambda k=k: nc.tensor.transpose(out=pT[:, k * 128:(k + 1) * 128], in_=hb[:, k * 128:(k + 1) * 128],
                                                     identity=self.identb[:]), reads=[hb, self.identb], writes=[pT])
        for k in range(8):
            P.op(ACT, lambda k=k: nc.scalar.activation(out=hT[:, k, dcol:dcol + (shi - slo)], in_=pT[:, k * 128 + slo:k * 128 + shi],
                                                       func=AF.Identity, bias=BT[:, k:k + 1], scale=AT[:, k:k + 1]),
                 reads=[pT, AT, BT], writes=[hT])

    def post_alloc(self, st):
        d = {}
        d["xr"] = [self.sb(st, "xr%d" % i, [128, D], F32) for i in range(2)]
        d["y"] = [self.sb(st, "ysb%d" % i, [128, D], F32) for i in range(1)]
        d["sq"] = self.junk
        d["ss"] = [self.sb(st, "pss%d" % i, [128, 1], F32) for i in range(2)]
        d["rstd"] = [self.sb(st, "prstd%d" % i, [128, 1], F32) for i in range(2)]
        d["r1"] = self.sb(st, "pr1", [128, 1], F32)
        d["r2"] = self.sb(st, "pr2", [128, 1], F32)
        d["r3"] = self.sb(st, "pr3", [128, 1], F32)
        d["i"] = 0
        return d

    def post_tile(self, po, pys, x_src, x_dst, j, G, bias_bc=None, next_j=None):
        nc, P = self.nc, self.P
        i = po["i"]
        po["i"] += 1
        xr = po["xr"][i % 2]
        y = po["y"][0]
        ss = po["ss"][i % 2]
        rstd = po["rstd"][i % 2]
        sq = po["sq"]
        if po.get("loaded") != j:
            P.dma(SP, xr[:], x_src[j * 128:(j + 1) * 128, :], writes=[xr])
        for h in range(2):
            if bias_bc is not None:
                P.op(DVE, lambda h=h: nc.vector.tensor_tensor(out=y[:, h * 512:(h + 1) * 512], in0=pys[h][:],
                                                             in1=bias_bc[:, h * 512:(h + 1) * 512], op=ALU.add),
                     reads=[pys[h], bias_bc], writes=[y])
            else:
                P.op(ACT, lambda h=h: nc.scalar.copy(out=y[:, h * 512:(h + 1) * 512], in_=pys[h][:]), reads=[pys[h]], writes=[y])
        P.op(ACT, lambda: nc.scalar.activation(out=sq[:], in_=y[:], func=AF.Square, accum_out=ss[:]), reads=[y], writes=[sq, ss])
        self.rsqrt_pool(rstd, ss, 1, 1.0 / D)
        P.op(DVE, lambda: nc.vector.scalar_tensor_tensor(out=y[:], in0=y[:], scalar=rstd[:, 0:1], in1=G[:], op0=ALU.mult, op1=ALU.mult),
             reads=[y, rstd, G], writes=[y])
        P.op(DVE, lambda: nc.vector.tensor_tensor(out=xr[:], in0=xr[:], in1=y[:], op=ALU.add), reads=[xr, y], writes=[xr])
        if next_j is not None:
            xn = po["xr"][(i + 1) % 2]
            P.dma(SP, xn[:], x_src[next_j * 128:(next_j + 1) * 128, :], writes=[xn])
            po["loaded"] = next_j
        P.dma(POOL, x_dst[j * 128:(j + 1) * 128, :], xr[:], reads=[xr])

    def conf_load(self, st, j):
        nc, P = self.nc, self.P
        cf = self.cf
        w = {}
        w["w1"] = self.sb(st, "w1b", [128, 8, 2 * D], BF16)
        w["w2"] = self.sb(st, "w2b", [128, 8, D], BF16)
        w["b1T"] = self.sb(st, "b1T", [128, 16], F32)
        w["hb1g"] = self.sb(st, "hb1g", [128, 8], F32)
        w["dwT"] = self.sb(st, "dwT", [128, 8, KW], F32)
        w["dwbT"] = self.sb(st, "dwbT", [128, 8], F32)
        w["lngT"] = self.sb(st, "lngT", [128, 8], F32)
        w["lnbT"] = self.sb(st, "lnbT", [128, 8], F32)
        w["b2"] = self.sb(st, "b2bc", [128, D], F32)
        self.cw = w
        with contextlib.ExitStack() as st2:
            self.stg = [self.sb(st2, "stg%d" % i, [128, 8, 512], F32) for i in range(3)]
            self.stg_i = 0
            self.conv_all = True
            self.load_w_bf16(st2, w["w1"], lambda k0, k1, c0, c1: w["w1"][:, k0:k1, c0:c1], cf["w1"][j], 8, 2 * D)
            self.load_w_bf16(st2, w["w2"], lambda k0, k1, c0, c1: w["w2"][:, k0:k1, c0:c1], cf["w2"][j], 8, D)
            P.dma(SP, w["b1T"][:], cf["b1T"][j], writes=[w["b1T"]])
            P.dma(SP, w["dwT"][:], cf["dwT"][j], writes=[w["dwT"]])
            P.dma(SP, w["dwbT"][:], cf["dwbT"][j], writes=[w["dwbT"]])
            P.dma(SP, w["lngT"][:], cf["lngT"][j], writes=[w["lngT"]])
            P.dma(SP, w["lnbT"][:], cf["lnbT"][j], writes=[w["lnbT"]])
            self.bc_load(w["b2"], cf["b2"][j])
            P.op(DVE, lambda: nc.vector.tensor_scalar(out=w["hb1g"][:], in0=w["b1T"][:, 8:16], scalar1=0.5, scalar2=None, op0=ALU.mult),
                 reads=[w["b1T"]], writes=[w["hb1g"]])
            P.op(DVE, lambda: nc.vector.tensor_scalar(out=w["dwT"][:], in0=w["dwT"][:], scalar1=0.5, scalar2=None, op0=ALU.mult),
                 reads=[w["dwT"]], writes=[w["dwT"]])
            dgb = [self.sb(st2, "dgb%d" % i, [128, KW, 128], BF16) for i in range(2)]
            for c in range(8):
                dg = dgb[c % 2]
                for tap in range(KW):
                    P.op(DVE, lambda c=c, tap=tap, dg=dg: nc.vector.tensor_scalar(
                        out=dg[:, tap, :], in0=self.identb[:], scalar1=w["dwT"][:, c, tap:tap + 1], scalar2=None, op0=ALU.mult),
                        reads=[self.identb, w["dwT"]], writes=[dg])
                P.dma(SP, self.cdiag[c], dg[:], reads=[dg])
            P.flush()

    def conf_pass(self, st0, l, j, x_src, x_dst, N, s, T):
        nc, P = self.nc, self.P
        w = self.cw
        HAL = 15
        TW = T + 2 * HAL
        nt_c = T // 128
        nt_h = nt_c + 2
        off = 128 - HAL
        half = TW // 2
        assert TW % 2 == 0 and half <= 512
        bank = self.bank
        pa = (bank[0], bank[1]); pg = (bank[2], bank[3]); pc = bank[4]; pT = self.pTb; py = (bank[5], bank[6])
        with contextlib.ExitStack() as st:
            po = self.post_alloc(st)
            AT, BT, G = self.mod_prep(st, l, s, "mix", tmp=po["y"][0])
            pn = self.prenorm_alloc(st, nt_h, nbuf=4)
            hT = pn["hT"]
            glu = self.sb(st, "glu", [128, 8, TW], BF16)
            tg = [self.sb(st, "tg%d" % i, [128, 2, half], F32) for i in range(2)]
            asb = [self.sb(st, "asb%d" % i, [128, 2, half], F32) for i in range(2)]
            diag = [self.sb(st, "diag%d" % i, [128, KW, 128], BF16) for i in range(3)]
            vsb = self.sb(st, "vsb", [128, 8, T], F32)
            vb = self.sb(st, "vb", [128, 8, T], BF16)
            vsq = self.sb(st, "vsq", [128, 8, T], BF16)
            mean = self.sb(st, "mean", [128, T], F32)
            m2 = self.sb(st, "m2", [128, T], F32)
            var = self.sb(st, "var", [128, T], F32)
            rs = self.sb(st, "rs", [128, T], F32)
            q1 = self.sb(st, "q1", [128, T], F32)
            q2 = self.sb(st, "q2", [128, T], F32)
            q3 = self.sb(st, "q3", [128, T], F32)
            vn = [self.sb(st, "vn%d" % i, [128, T], F32) for i in range(2)]
            uT = self.sb(st, "uT", [128, 8, T], BF16)
            nsteps = N // T
            ntiles = N // 128
            dctr = [0]

            def pre(t, pos, part="AB", tok=None):
                jt = t * nt_c - 1 + pos
                jj = jt if 0 <= jt < ntiles else None
                if part == "AB":
                    self.prenorm_tile(pn, x_src, jj, pos, AT, BT, pT)
                elif part == "L":
                    return self.prenorm_load(pn, x_src, jj)
                elif part == "S":
                    return self.prenorm_stat(pn, tok, pos)
                else:
                    self.prenorm_B(pn, tok, AT, BT, pT)

            def w1glu(t, c):
                tgc = tg[c % 2]; ac = asb[c % 2]
                for hf in range(2):
                    for k in range(8):
                        P.op(PE, lambda k=k, hf=hf, c=c: nc.tensor.matmul(
                            pa[hf][:, 0:half], lhsT=w["w1"][:, k, c * 128:(c + 1) * 128],
                            rhs=hT[:, k, off + hf * half: off + (hf + 1) * half], start=(k == 0), stop=(k == 7)),
                            reads=[w["w1"], hT], writes=[pa[hf]])
                for hf in range(2):
                    for k in range(8):
                        P.op(PE, lambda k=k, hf=hf, c=c: nc.tensor.matmul(
                            pg[hf][:, 0:half], lhsT=w["w1"][:, k, D + c * 128: D + (c + 1) * 128],
                            rhs=hT[:, k, off + hf * half: off + (hf + 1) * half], start=(k == 0), stop=(k == 7)),
                            reads=[w["w1"], hT], writes=[pg[hf]])
                for hf in range(2):
                    P.op(ACT, lambda hf=hf, c=c, tgc=tgc: nc.scalar.activation(
                        out=tgc[:, hf, :], in_=pg[hf][:, 0:half], func=AF.Tanh, bias=w["hb1g"][:, c:c + 1], scale=0.5),
                        reads=[pg[hf], w["hb1g"]], writes=[tgc])
                    P.op(ACT, lambda hf=hf, c=c, ac=ac: nc.scalar.activation(
                        out=ac[:, hf, :], in_=pa[hf][:, 0:half], func=AF.Identity, bias=w["b1T"][:, c:c + 1], scale=1.0),
                        reads=[pa[hf], w["b1T"]], writes=[ac])
                P.op(DVE, lambda c=c, tgc=tgc, ac=ac: nc.vector.scalar_tensor_tensor(
                    out=glu[:, c, :], in0=tgc[:].rearrange("p a b -> p (a b)"), scalar=1.0,
                    in1=ac[:].rearrange("p a b -> p (a b)"), op0=ALU.add, op1=ALU.mult),
                    reads=[tgc, ac], writes=[glu])
                if t == 0:
                    P.op(POOL, lambda c=c: nc.gpsimd.memset(glu[:, c, 0:HAL], 0.0), writes=[glu])
                if t == nsteps - 1:
                    P.op(POOL, lambda c=c: nc.gpsimd.memset(glu[:, c, TW - HAL:TW], 0.0), writes=[glu])

            def diag_load(n):
                if n < nsteps * 8:
                    dg_ = diag[n % 3]
                    P.dma(POOL, dg_[:], self.cdiag[n % 8], writes=[dg_])

            def conv(t, c):
                n = t * 8 + c
                dg = diag[n % 3]
                if n == 0:
                    for m in range(3):
                        diag_load(m)
                for tap in range(KW):
                    P.op(PE, lambda c=c, tap=tap, dg=dg: nc.tensor.matmul(
                        pc[:, 0:T], lhsT=dg[:, tap, :], rhs=glu[:, c, tap:tap + T], start=(tap == 0), stop=(tap == KW - 1)),
                        reads=[dg, glu], writes=[pc])
                diag_load(n + 3)
                P.op(ACT, lambda c=c: nc.scalar.activation(out=vsb[:, c, :], in_=pc[:, 0:T], func=AF.Identity,
                                                           bias=w["dwbT"][:, c:c + 1], scale=1.0),
                     reads=[pc, w["dwbT"]], writes=[vsb])
                P.op(DVE, lambda c=c: nc.vector.tensor_copy(out=vb[:, c, :], in_=vsb[:, c, :]), reads=[vsb], writes=[vb])
                P.op(ACT, lambda c=c: nc.scalar.activation(out=vsq[:, c, :], in_=vsb[:, c, :], func=AF.Square),
                     reads=[vsb], writes=[vsq])

            def stats(t):
                ps1, ps2 = pa[0], pg[0]
                for c in range(8):
                    P.op(PE, lambda c=c: nc.tensor.matmul(ps1[:, 0:T], lhsT=self.onesb[:], rhs=vb[:, c, :], start=(c == 0), stop=(c == 7)),
                         reads=[self.onesb, vb], writes=[ps1])
                for c in range(8):
                    P.op(PE, lambda c=c: nc.tensor.matmul(ps2[:, 0:T], lhsT=self.onesb[:], rhs=vsq[:, c, :], start=(c == 0), stop=(c == 7)),
                         reads=[self.onesb, vsq], writes=[ps2])
                P.op(ACT, lambda: nc.scalar.copy(out=mean[:], in_=ps1[:, 0:T]), reads=[ps1], writes=[mean])
                P.op(DVE, lambda: nc.vector.tensor_tensor(out=m2[:], in0=mean[:], in1=mean[:], op=ALU.mult), reads=[mean], writes=[m2])
                P.op(DVE, lambda: nc.vector.scalar_tensor_tensor(out=var[:], in0=ps2[:, 0:T], scalar=EPS, in1=m2[:], op0=ALU.add,
                                                                 op1=ALU.subtract), reads=[ps2, m2], writes=[var])
                self.rsqrt(rs[:], var[:], var, rs, q1, q2, q3, slice(None), iters=2)

            def normalize(t, c):
                v_ = vn[c % 2]
                P.op(DVE, lambda c=c, v_=v_: nc.vector.tensor_tensor(out=v_[:], in0=vsb[:, c, :], in1=mean[:], op=ALU.subtract),
                     reads=[vsb, mean], writes=[v_])
                P.op(DVE, lambda c=c, v_=v_: nc.vector.tensor_tensor(out=v_[:], in0=v_[:], in1=rs[:], op=ALU.mult),
                     reads=[v_, rs], writes=[v_])
                P.op(ACT, lambda c=c, v_=v_: nc.scalar.activation(out=uT[:, c, :], in_=v_[:], func=AF.Silu,
                                                                  bias=w["lnbT"][:, c:c + 1], scale=w["lngT"][:, c:c + 1]),
                     reads=[v_, w["lnbT"], w["lngT"]], writes=[uT])

            def w2post(t, tt):
                for h in range(2):
                    for c in range(8):
                        P.op(PE, lambda c=c, h=h, tt=tt: nc.tensor.matmul(
                            py[h][:], lhsT=uT[:, c, tt * 128:(tt + 1) * 128], rhs=w["w2"][:, c, h * 512:(h + 1) * 512],
                            start=(c == 0), stop=(c == 7)), reads=[uT, w["w2"]], writes=[py[h]])
                jn = t * nt_c + tt + 1
                self.post_tile(po, py, x_src, x_dst, t * nt_c + tt, G, bias_bc=w["b2"], next_j=jn if jn < ntiles else None)

            def pre_pipeline(t, bodies):
                LA = len(pn["xt"]) - 1
                lt, tk = {}, {}
                for i in range(min(LA, nt_h)):
                    lt[i] = pre(t, i, "L")
                tk[0] = pre(t, 0, "S", lt[0])
                for i in range(max(nt_h, len(bodies))):
                    if i + LA < nt_h:
                        lt[i + LA] = pre(t, i + LA, "L")
                    if i + 1 < nt_h:
                        tk[i + 1] = pre(t, i + 1, "S", lt[i + 1])
                    if i < len(bodies):
                        bodies[i]()
                    if i < nt_h:
                        pre(t, i, "B", tk[i])

            pre_pipeline(0, [])
            for c in range(8):
                w1glu(0, c)
            for t in range(nsteps):
                for c in range(8):
                    conv(t, c)
                stats(t)
                nxt = t + 1 < nsteps
                bodies = [(lambda i=i: normalize(t, i)) for i in range(8)]
                if nxt:
                    pre_pipeline(t + 1, bodies)
                else:
                    for b in bodies:
                        b()
                for i in range(8):
                    if nxt:
                        w1glu(t + 1, i)
                    if nt_c == 4:
                        if i % 2 == 1:
                            w2post(t, i // 2)
                    elif i < nt_c:
                        w2post(t, i)
            P.flush()

    def ffn_load(self, st, l):
        nc, P = self.nc, self.P
        ff = self.ff
        w = {}
        w["wa"] = self.sb(st, "wab", [128, 8, FF], BF16)
        w["wb"] = self.sb(st, "wbb", [128, 8, FF], BF16)
        w["wo"] = self.sb(st, "wob", [128, NFC, D], BF16)
        w["dwT"] = self.sb(st, "fdwT", [128, NFC, 9], F32)
        w["dwbT"] = self.sb(st, "fdwbT", [128, NFC], F32)
        self.fw = w
        with contextlib.ExitStack() as st2:
            self.stg = [self.sb(st2, "stg%d" % i, [128, 8, 512], F32) for i in range(3)]
            self.stg_i = 0
            self.conv_all = True
            self.load_w_bf16(st2, w["wa"], lambda k0, k1, c0, c1: w["wa"][:, k0:k1, c0:c1], ff["wa"][l], 8, FF)
            self.load_w_bf16(st2, w["wb"], lambda k0, k1, c0, c1: w["wb"][:, k0:k1, c0:c1], ff["wb"][l], 8, FF)
            self.load_w_bf16(st2, w["wo"], lambda k0, k1, c0, c1: w["wo"][:, k0:k1, c0:c1], ff["wo"][l], NFC, D)
            P.dma(SP, w["dwT"][:], ff["dwT"][l], writes=[w["dwT"]])
            P.dma(SP, w["dwbT"][:], ff["dwbT"][l], writes=[w["dwbT"]])
            dgb = [self.sb(st2, "fdgb%d" % i, [128, 9, 128], BF16) for i in range(2)]
            for c in range(NFC):
                dg = dgb[c % 2]
                for tap in range(9):
                    P.op(DVE, lambda c=c, tap=tap, dg=dg: nc.vector.tensor_scalar(
                        out=dg[:, tap, :], in0=self.identb[:], scalar1=w["dwT"][:, c, tap:tap + 1], scalar2=None, op0=ALU.mult),
                        reads=[self.identb, w["dwT"]], writes=[dg])
                P.dma(SP, self.fdiag[c], dg[:], reads=[dg])
            P.flush()

    def ffn_pass(self, st0, l, x_src, x_dst, N, s, T, grid):
        nc, P = self.nc, self.P
        w = self.fw
        bank = self.bank
        nt_c = T // 128
        if grid:
            W, R, hal = 64, T // 64, 1
            taps = [(dr, dc, (dr + 1) * 3 + (dc + 1)) for dr in (-1, 0, 1) for dc in (-1, 0, 1)]
        else:
            W, R, hal = T, 1, 0
            taps = [(0, dc, 3 + (dc + 1)) for dc in (-1, 0, 1)]
        nt_h = nt_c + 2 * hal
        RA = R + 2 * hal
        acols = RA * W
        aoff = 0
        goff = hal * 64
        hcols = T + 2 * hal * 64
        nsplit = 2 if acols > 512 else 1
        rows_sp = RA // nsplit
        csp = rows_sp * W
        assert rows_sp * nsplit == RA and csp <= 512
        ntap = len(taps)
        pa = (bank[0], bank[1]); pgs = (bank[2], bank[3]); pcs = (bank[4], bank[5]); pT = self.pTb; py = (bank[6], bank[0])
        ntiles = N // 128
        with contextlib.ExitStack() as st:
            po = self.post_alloc(st)
            AT, BT, G = self.mod_prep(st, l, s, "ffn", tmp=po["y"][0])
            pn = self.prenorm_alloc(st, hcols)
            hT = pn["hT"]
            apad = [self.sb(st, "apad%d" % i, [128, RA, W + 2], BF16) for i in range(2)]
            ssb = [self.sb(st, "ssb%d" % i, [128, T], BF16) for i in range(2)]
            NFD = 2
            fdg = [self.sb(st, "fdg%d" % i, [128, 9, 128], BF16) for i in range(NFD)]
            KDVE = 4 if ntap == 9 else 0
            acc = [self.sb(st, "acc%d" % i, [128, T], F32) for i in range(1)] if KDVE else None
            u = self.sb(st, "u", [128, NFC, T], BF16)
            for ap_ in apad:
                P.op(POOL, lambda ap_=ap_: nc.gpsimd.memset(ap_[:], 0.0), writes=[ap_])
            nsteps = N // T

            def prenorm_pos(t, pos, part="AB", tok=None):
                jt = t * nt_c - hal + pos
                if hal == 0:
                    cm = (pos * 128, 0, 128)
                elif pos == 0:
                    cm = (0, 64, 128)
                elif pos == nt_h - 1:
                    cm = (64 + nt_c * 128, 0, 64)
                else:
                    cm = (64 + (pos - 1) * 128, 0, 128)
                jj = jt if 0 <= jt < ntiles else None
                if part == "AB":
                    self.prenorm_tile(pn, x_src, jj, pos, AT, BT, pT, cmap=cm)
                elif part == "L":
                    return self.prenorm_load(pn, x_src, jj)
                elif part == "S":
                    return self.prenorm_stat(pn, tok, pos, cmap=cm)
                else:
                    self.prenorm_B(pn, tok, AT, BT, pT)

            def ab_mm(c):
                ap_ = apad[c % 2]; pg = pgs[c % 2]
                for sp in range(nsplit):
                    for k in range(8):
                        P.op(PE, lambda k=k, sp=sp, c=c: nc.tensor.matmul(
                            pa[sp][:, 0:csp], lhsT=w["wa"][:, k, c * 128:(c + 1) * 128],
                            rhs=hT[:, k, aoff + sp * csp: aoff + (sp + 1) * csp], start=(k == 0), stop=(k == 7)),
                            reads=[w["wa"], hT], writes=[pa[sp]])
                for k in range(8):
                    P.op(PE, lambda k=k, c=c, pg=pg: nc.tensor.matmul(
                        pg[:, 0:T], lhsT=w["wb"][:, k, c * 128:(c + 1) * 128],
                        rhs=hT[:, k, goff: goff + T], start=(k == 0), stop=(k == 7)),
                        reads=[w["wb"], hT], writes=[pg])
                for sp in range(nsplit):
                    P.op(ACT, lambda sp=sp, ap_=ap_: nc.scalar.copy(
                        out=ap_[:, sp * rows_sp:(sp + 1) * rows_sp, 1:W + 1],
                        in_=pa[sp][:, 0:csp].rearrange("p (r w) -> p r w", w=W)), reads=[pa[sp]], writes=[ap_])

            dctr = [0]

            def fdiag_load(n):
                if n < nsteps * NFC:
                    dg_ = fdg[n % NFD]
                    P.dma(POOL, dg_[:], self.fdiag[n % NFC], writes=[dg_])

            def conv(c):
                ap_ = apad[c % 2]; pg = pgs[c % 2]; pc = pcs[c % 2]; sb_ = ssb[c % 2]
                n = dctr[0]
                dctr[0] += 1
                dg = fdg[n % NFD]
                if n == 0:
                    for m in range(NFD):
                        fdiag_load(m)
                pcv = pc[:, 0:T].rearrange("p (r w) -> p r w", w=W)
                n_pe = ntap - KDVE
                pe_taps = taps[:n_pe]; dve_taps = taps[n_pe:]
                ac_ = acc[0] if KDVE else None
                if KDVE:
                    accv = ac_[:].rearrange("p (r w) -> p r w", w=W)
                    for ti, (dr, dc, wi) in enumerate(dve_taps):
                        src = ap_[:, hal + dr: hal + dr + R, 1 + dc: 1 + dc + W]
                        if ti == 0:
                            P.op(DVE, lambda src=src, accv=accv, c=c, wi=wi: nc.vector.tensor_scalar(
                                out=accv, in0=src, scalar1=w["dwT"][:, c, wi:wi + 1], scalar2=None, op0=ALU.mult),
                                reads=[ap_, w["dwT"]], writes=[ac_])
                        else:
                            P.op(DVE, lambda src=src, accv=accv, c=c, wi=wi: nc.vector.scalar_tensor_tensor(
                                out=accv, in0=src, scalar=w["dwT"][:, c, wi:wi + 1], in1=accv, op0=ALU.mult, op1=ALU.add),
                                reads=[ap_, w["dwT"], ac_], writes=[ac_])
                for ti, (dr, dc, wi) in enumerate(pe_taps):
                    src = ap_[:, hal + dr: hal + dr + R, 1 + dc: 1 + dc + W]
                    P.op(PE, lambda src=src, pcv=pcv, dg=dg, wi=wi, ti=ti: nc.tensor.matmul(
                        pcv, lhsT=dg[:, wi, :], rhs=src, start=(ti == 0), stop=(ti == n_pe - 1)),
                        reads=[dg, ap_], writes=[pc])
                fdiag_load(n + NFD)
                if KDVE:
                    P.op(DVE, lambda ac_=ac_, pc=pc: nc.vector.tensor_tensor(out=ac_[:], in0=ac_[:], in1=pc[:, 0:T], op=ALU.add),
                         reads=[ac_, pc], writes=[ac_])
                    P.op(ACT, lambda c=c, ac_=ac_, sb_=sb_: nc.scalar.activation(out=sb_[:], in_=ac_[:], func=AF.Silu,
                                                                               bias=w["dwbT"][:, c:c + 1], scale=1.0),
                         reads=[ac_, w["dwbT"]], writes=[sb_])
                else:
                    P.op(ACT, lambda c=c, pc=pc, sb_=sb_: nc.scalar.activation(out=sb_[:], in_=pc[:, 0:T], func=AF.Silu,
                                                                             bias=w["dwbT"][:, c:c + 1], scale=1.0),
                         reads=[pc, w["dwbT"]], writes=[sb_])
                P.op(DVE, lambda c=c, sb_=sb_, pg=pg: nc.vector.tensor_tensor(out=u[:, c, :], in0=sb_[:], in1=pg[:, 0:T], op=ALU.mult),
                     reads=[sb_, pg], writes=[u])

            def wo_mm(t, tt):
                for h in range(2):
                    for c in range(NFC):
                        P.op(PE, lambda c=c, h=h, tt=tt: nc.tensor.matmul(
                            py[h][:], lhsT=u[:, c, tt * 128:(tt + 1) * 128], rhs=w["wo"][:, c, h * 512:(h + 1) * 512],
                            start=(c == 0), stop=(c == NFC - 1)), reads=[u, w["wo"]], writes=[py[h]])

            def wo_post(t, tt):
                jn = t * nt_c + tt + 1
                self.post_tile(po, py, x_src, x_dst, t * nt_c + tt, G, next_j=jn if jn < ntiles else None)

            def pre_pipeline(t, pre_bodies, post_bodies):
                lt, tk = {}, {}
                for i in range(min(1, nt_h)):
                    lt[i] = prenorm_pos(t, i, "L")
                tk[0] = prenorm_pos(t, 0, "S", lt[0])
                for i in range(max(nt_h, len(pre_bodies))):
                    if i + 1 < nt_h:
                        lt[i + 1] = prenorm_pos(t, i + 1, "L")
                        tk[i + 1] = prenorm_pos(t, i + 1, "S", lt[i + 1])
                    if i < len(pre_bodies):
                        pre_bodies[i]()
                    if i < nt_h:
                        prenorm_pos(t, i, "B", tk[i])
                    if i < len(post_bodies):
                        post_bodies[i]()

            pre_pipeline(0, [], [])
            for t in range(nsteps):
                for c in range(NFC + 1):
                    if c < NFC:
                        ab_mm(c)
                    if c >= 1:
                        conv(c - 1)
                nxt = t + 1 < nsteps
                mm_b = [(lambda i=i: wo_mm(t, i)) for i in range(nt_c)]
                po_b = [(lambda i=i: wo_post(t, i)) for i in range(nt_c)]
                if nxt:
                    pre_pipeline(t + 1, mm_b, po_b)
                else:
                    for i in range(nt_c):
                        mm_b[i]()
                        po_b[i]()
            P.flush()

    def gla_load(self, st, j):
        nc, P = self.nc, self.P
        gl = self.gl
        w = {}
        w["wq"] = self.sb(st, "wqb", [128, 8, 512], BF16)
        w["wk"] = self.sb(st, "wkb", [128, 8, 512], BF16)
        w["wv"] = self.sb(st, "wvb", [128, 8, D], BF16)
        w["wr"] = self.sb(st, "wrb", [128, 8, D], BF16)
        w["wo"] = self.sb(st, "wob", [128, 8, D], BF16)
        w["wg1"] = self.sb(st, "wg1b", [128, 8, 32], BF16)
        w["wg2"] = self.sb(st, "wg2b", [32, D], BF16)
        w["bg"] = self.sb(st, "bgbc", [128, D], F32)
        w["ng"] = self.sb(st, "ngbc", [128, D], F32)
        self.S = [[self.sb(st, "S%d_%d" % (d_, h), [128, 256], F32) for h in range(4)] for d_ in range(2)]
        self.tri = self.sb(st, "tri", [128, 2, 128], F32)
        P.dma(SP, self.tri[:], self.consts[:, 2:4, :], writes=[self.tri])
        self.gw = w
        with contextlib.ExitStack() as st2:
            self.stg = [self.sb(st2, "stg%d" % i, [128, 8, 512], F32) for i in range(3)]
            self.stg_i = 0
            self.conv_all = True
            for nm, ncol in (("wq", 512), ("wk", 512), ("wv", D), ("wr", D), ("wo", D), ("wg1", 32)):
                self.load_w_bf16(st2, w[nm], lambda k0, k1, c0, c1, nm=nm: w[nm][:, k0:k1, c0:c1], gl[nm][j], 8, ncol)
            g2f = self.sb(st2, "g2f", [32, D], F32)
            P.dma(SP, g2f[:], gl["wg2"][j], writes=[g2f])
            P.op(DVE, lambda: nc.vector.tensor_copy(out=w["wg2"][:], in_=g2f[:]), reads=[g2f], writes=[w["wg2"]])
            self.bc_load(w["bg"], gl["bg"][j])
            self.bc_load(w["ng"], gl["ng"][j])
            P.flush()

    def gla_prep(self, st, d_, T, lsb, qT_sb, kT_sb, bufs):
        nc, P = self.nc, self.P
        nt_c = T // 128
        nch = T // 64
        khT, qhT, edec, Dsbs, EKs, EQs, Lends = bufs
        bank = self.bank
        endcol = 63 if d_ == 0 else 0
        for h in range(4):
            LT = bank[5 + (h % 2)]
            Dsb, EK, EQ, Lend = Dsbs[h % 2], EKs[h % 2], EQs[h % 2], Lends[h % 2]
            for tt in range(nt_c):
                P.op(PE, lambda tt=tt, h=h, LT=LT: nc.tensor.matmul(LT[:, tt * 128:(tt + 1) * 128], lhsT=lsb[:, tt, h * 128:(h + 1) * 128],
                                                                  rhs=self.tri[:, d_, :], start=True, stop=True),
                     reads=[lsb, self.tri], writes=[LT])
            LTv = LT[:, 0:T].rearrange("p (c j) -> p c j", j=64)
            P.op(ACT, lambda LTv=LTv, Lend=Lend: nc.scalar.copy(out=Lend[:, 0:nch], in_=LTv[:, :, endcol]), reads=[LT], writes=[Lend])
            P.op(DVE, lambda LTv=LTv, Lend=Lend, Dsb=Dsb: nc.vector.tensor_tensor(out=Dsb[:, 0:T].rearrange("p (c j) -> p c j", j=64), in0=LTv,
                                                             in1=Lend[:, 0:nch].unsqueeze(2).to_broadcast([128, nch, 64]), op=ALU.subtract),
                 reads=[LT, Lend], writes=[Dsb])
            P.op(ACT, lambda EK=EK, Dsb=Dsb: nc.scalar.activation(out=EK[:, 0:T], in_=Dsb[:, 0:T], func=AF.Exp), reads=[Dsb], writes=[EK])
            P.op(ACT, lambda EQ=EQ, Dsb=Dsb: nc.scalar.activation(out=EQ[:, 0:T], in_=Dsb[:, 0:T], func=AF.Exp, scale=-1.0), reads=[Dsb], writes=[EQ])
            P.op(ACT, lambda h=h, Lend=Lend: nc.scalar.activation(out=edec[:, h, 0:nch], in_=Lend[:, 0:nch], func=AF.Exp, scale=-1.0),
                 reads=[Lend], writes=[edec])
            P.op(DVE, lambda h=h, EK=EK: nc.vector.tensor_tensor(out=khT[:, h, 0:T], in0=kT_sb[:, h, 0:T], in1=EK[:, 0:T], op=ALU.mult),
                 reads=[kT_sb, EK], writes=[khT])
            P.op(DVE, lambda h=h, EQ=EQ: nc.vector.tensor_tensor(out=qhT[:, h, 0:T], in0=qT_sb[:, h, 0:T], in1=EQ[:, 0:T], op=ALU.mult),
                 reads=[qT_sb, EQ], writes=[qhT])

    def gla_scan_tile(self, d_, tt, khT, qhT, edec, v_sb, sc, want_o):
        nc, P = self.nc, self.P
        bank = self.bank
        po = (bank[0], bank[1]); pP = (bank[2], bank[3]); psc = bank[4]; pT = self.pTb
        ktoks, PTs, Sbfs = sc
        ktok = ktoks[tt % 2]; PT = PTs[tt % 2]
        S = self.S[d_]
        c0, c1 = tt * 128, (tt + 1) * 128
        for h in range(4):
            P.op(PE, lambda h=h: nc.tensor.transpose(out=pT[:, h * 128:(h + 1) * 128], in_=khT[:, h, c0:c1], identity=self.identb[:]),
                 reads=[khT, self.identb], writes=[pT])
        P.op(ACT, lambda: nc.scalar.copy(out=ktok[:].rearrange("p h d -> p (h d)"), in_=pT[:, 0:512]), reads=[pT], writes=[ktok])
        if want_o:
            for h in range(4):
                P.op(PE, lambda h=h: nc.tensor.matmul(psc[:, h * 128:(h + 1) * 128], lhsT=khT[:, h, c0:c1], rhs=qhT[:, h, c0:c1],
                                                      start=True, stop=True), reads=[khT, qhT], writes=[psc])
            P.op(DVE, lambda: nc.vector.tensor_tensor(out=PT[:], in0=psc[:].rearrange("p (h t) -> p h t", h=4),
                                                     in1=self.maskfb[:, d_, :].unsqueeze(1).to_broadcast([128, 4, 128]), op=ALU.mult),
                 reads=[psc, self.maskfb], writes=[PT])
            for h in range(4):
                pb = po[h // 2]
                oc = (h % 2) * 256
                P.op(PE, lambda h=h, pb=pb, oc=oc: nc.tensor.matmul(pb[:, oc:oc + 256], lhsT=PT[:, h, :], rhs=v_sb[:, tt, h * 256:(h + 1) * 256],
                                                                   start=(h % 2 == 0), stop=False), reads=[PT, v_sb], writes=[pb])
        order = (0, 1) if d_ == 0 else (1, 0)
        for ci, ch in enumerate(order):
            r0, r1 = ch * 64, (ch + 1) * 64
            gch = tt * 2 + ch
            for h in range(4):
                pb = pP[h // 2]
                oc = (h % 2) * 256
                P.op(PE, lambda h=h, pb=pb, oc=oc, r0=r0, r1=r1: nc.tensor.matmul(pb[:, oc:oc + 256], lhsT=ktok[r0:r1, h, :],
                                                                   rhs=v_sb[r0:r1, tt, h * 256:(h + 1) * 256], start=True, stop=True),
                     reads=[ktok, v_sb], writes=[pb])
            if want_o:
                for h in range(4):
                    Sh = S[h]; Sb = Sbfs[ci][h]
                    P.op(ACT, lambda h=h, Sh=Sh, Sb=Sb, gch=gch: nc.scalar.activation(out=Sb[:], in_=Sh[:], func=AF.Identity, scale=edec[:, h, gch:gch + 1]),
                         reads=[Sh, edec], writes=[Sb])
                for h in range(4):
                    Sb = Sbfs[ci][h]
                    pb = po[h // 2]
                    oc = (h % 2) * 256
                    P.op(PE, lambda h=h, pb=pb, oc=oc, r0=r0, r1=r1, ci=ci, Sb=Sb: nc.tensor.matmul(pb[r0:r1, oc:oc + 256], lhsT=qhT[:, h, c0 + r0:c0 + r1], rhs=Sb[:],
                                                                       start=False, stop=(ci == 1)), reads=[qhT, Sb], writes=[pb])
            for h in range(4):
                Sh = S[h]
                pb2 = pP[h // 2]
                oc2 = (h % 2) * 256
                P.op(DVE, lambda h=h, Sh=Sh, pb2=pb2, oc2=oc2, gch=gch: nc.vector.scalar_tensor_tensor(
                    out=Sh[:], in0=Sh[:], scalar=edec[:, h, gch:gch + 1], in1=pb2[:, oc2:oc2 + 256], op0=ALU.mult, op1=ALU.add),
                    reads=[Sh, edec, pb2], writes=[Sh])
        return po

    def gla_seq(self, st0, l, j, x_src, x_dst, N, s, T, zero_state, want_out):
        nc, P = self.nc, self.P
        w = self.gw
        bank = self.bank
        gs = self.gscr
        nt_c = T // 128
        nch = T // 64
        nsteps = N // T
        pT = self.pTb
        with contextlib.ExitStack() as st:
            AT, BT, G = self.mod_prep(st, l, s, "mix")
            pn = self.prenorm_alloc(st, nt_c)
            hT = pn["hT"]
            qT_sb = self.sb(st, "qT_sb", [128, 4, T], F32)
            kT_sb = self.sb(st, "kT_sb", [128, 4, T], F32)
            v_sb = self.sb(st, "v_sb", [128, nt_c, D], BF16)
            uT = self.sb(st, "uT", [32, T], BF16)
            lf = self.sb(st, "lf", [128, nt_c, 512], F32)
            lbt = [self.sb(st, "lbt%d" % i, [128, 512], F32) for i in range(2)]
            zt = self.sb(st, "zt", [128, D], F32)
            khT = self.sb(st, "khT", [128, 4, T], BF16)
            qhT = self.sb(st, "qhT", [128, 4, T], BF16)
            edec = self.sb(st, "edec", [128, 4, nch], F32)
            Dsb = [self.sb(st, "Dsb%d" % i, [128, T], F32) for i in range(2)]
            EK = [self.sb(st, "EK%d" % i, [128, T], F32) for i in range(2)]
            EQ = [self.sb(st, "EQ%d" % i, [128, T], F32) for i in range(2)]
            Lend = [self.sb(st, "Lend%d" % i, [128, nch], F32) for i in range(2)]
            ktok = [self.sb(st, "ktok%d" % i, [128, 4, 128], BF16) for i in range(2)]
            PT = [self.sb(st, "PT%d" % i, [128, 4, 128], BF16) for i in range(2)]
            Sbf = [[self.sb(st, "Sbf%d_%d" % (i, h), [128, 256], BF16) for h in range(4)] for i in range(2)]
            of_sb = [self.sb(st, "of_sb%d" % i, [128, D], F32) for i in range(2)]
            if zero_state:
                for h in range(4):
                    P.op(POOL, lambda h=h: nc.gpsimd.memset(self.S[0][h][:], 0.0), writes=[self.S[0][h]])
            for t in range(nsteps):
                t0 = t * T
                self.prenorm_many(pn, x_src, [(t * nt_c + pos, pos) for pos in range(nt_c)], AT, BT, pT)
                for which, wn, dst, scl in (("q", "wq", qT_sb, 128.0 ** -0.5), ("k", "wk", kT_sb, 1.0)):
                    for h in range(4):
                        pb = bank[h % 2]
                        for k in range(8):
                            P.op(PE, lambda k=k, h=h, pb=pb, wn=wn: nc.tensor.matmul(pb[:, 0:T], lhsT=w[wn][:, k, h * 128:(h + 1) * 128], rhs=hT[:, k, 0:T],
                                                                                   start=(k == 0), stop=(k == 7)), reads=[w[wn], hT], writes=[pb])
                        P.op(ACT, lambda h=h, pb=pb, dst=dst, scl=scl: nc.scalar.mul(out=dst[:, h, :], in_=pb[:, 0:T], mul=scl), reads=[pb], writes=[dst])
                        P.dma(SP, gs[which + "T"][h, :, t0:t0 + T], dst[:, h, :], reads=[dst])
                pu = bank[2]
                for k in range(8):
                    P.op(PE, lambda k=k: nc.tensor.matmul(pu[0:32, 0:T], lhsT=w["wg1"][:, k, :], rhs=hT[:, k, 0:T], start=(k == 0), stop=(k == 7)),
                         reads=[w["wg1"], hT], writes=[pu])
                P.op(ACT, lambda: nc.scalar.copy(out=uT[:, 0:T], in_=pu[0:32, 0:T]), reads=[pu], writes=[uT])
                for tt in range(nt_c):
                    pv = (bank[3], bank[4])
                    for hf in range(2):
                        for k in range(8):
                            P.op(PE, lambda k=k, hf=hf, tt=tt: nc.tensor.matmul(pv[hf][:], lhsT=hT[:, k, tt * 128:(tt + 1) * 128], rhs=w["wv"][:, k, hf * 512:(hf + 1) * 512],
                                                                               start=(k == 0), stop=(k == 7)), reads=[hT, w["wv"]], writes=[pv[hf]])
                        P.op(DVE, lambda hf=hf, tt=tt: nc.vector.tensor_copy(out=v_sb[:, tt, hf * 512:(hf + 1) * 512], in_=pv[hf][:]), reads=[pv[hf]], writes=[v_sb])
                    P.dma(SP, gs["v"][t0 + tt * 128:t0 + (tt + 1) * 128, :], v_sb[:, tt, :], reads=[v_sb])
                    pz = (bank[5], bank[6])
                    for hf in range(2):
                        P.op(PE, lambda hf=hf, tt=tt: nc.tensor.matmul(pz[hf][:], lhsT=uT[:, tt * 128:(tt + 1) * 128], rhs=w["wg2"][:, hf * 512:(hf + 1) * 512],
                                                                      start=True, stop=True), reads=[uT, w["wg2"]], writes=[pz[hf]])
                        P.op(DVE, lambda hf=hf: nc.vector.tensor_tensor(out=zt[:, hf * 512:(hf + 1) * 512], in0=pz[hf][:], in1=w["bg"][:, hf * 512:(hf + 1) * 512], op=ALU.add),
                             reads=[pz[hf], w["bg"]], writes=[zt])
                    P.op(ACT, lambda: nc.scalar.activation(out=zt[:], in_=zt[:], func=AF.Exp, scale=-1.0), reads=[zt], writes=[zt])
                    lb_ = lbt[tt % 2]
                    P.op(ACT, lambda tt=tt: nc.scalar.activation(out=lf[:, tt, :], in_=zt[:, 0:512], func=AF.Ln, bias=1.0, scale=1.0), reads=[zt], writes=[lf])
                    P.op(ACT, lambda lb_=lb_: nc.scalar.activation(out=lb_[:], in_=zt[:, 512:1024], func=AF.Ln, bias=1.0, scale=1.0), reads=[zt], writes=[lb_])
                    P.dma(SP, gs["lb"][t0 + tt * 128:t0 + (tt + 1) * 128, :], lb_[:], reads=[lb_])
                self.gla_prep(st, 0, T, lf, qT_sb, kT_sb, (khT, qhT, edec, Dsb, EK, EQ, Lend))
                for tt in range(nt_c):
                    po = self.gla_scan_tile(0, tt, khT, qhT, edec, v_sb, (ktok, PT, Sbf), want_out)
                    if want_out:
                        ob = of_sb[tt % 2]
                        for hf in range(2):
                            P.op(ACT, lambda hf=hf, ob=ob: nc.scalar.copy(out=ob[:, hf * 512:(hf + 1) * 512], in_=po[hf][:]), reads=[po[hf]], writes=[ob])
                        P.dma(SP, gs["of"][t0 + tt * 128:t0 + (tt + 1) * 128, :], ob[:], reads=[ob])
            P.flush()
        with contextlib.ExitStack() as st:
            if want_out:
                po_ = self.post_alloc(st)
                AT, BT, G = self.mod_prep(st, l, s, "mix", tmp=po_["y"][0])
                pn = self.prenorm_alloc(st, nt_c)
                hT = pn["hT"]
            qT_sb = self.sb(st, "qT_sb", [128, 4, T], F32)
            kT_sb = self.sb(st, "kT_sb", [128, 4, T], F32)
            v_sb = self.sb(st, "v_sb", [128, nt_c, D], BF16)
            lb = self.sb(st, "lb", [128, nt_c, 512], F32)
            khT = self.sb(st, "khT", [128, 4, T], BF16)
            qhT = self.sb(st, "qhT", [128, 4, T], BF16)
            edec = self.sb(st, "edec", [128, 4, nch], F32)
            Dsb = [self.sb(st, "Dsb%d" % i, [128, T], F32) for i in range(2)]
            EK = [self.sb(st, "EK%d" % i, [128, T], F32) for i in range(2)]
            EQ = [self.sb(st, "EQ%d" % i, [128, T], F32) for i in range(2)]
            Lend = [self.sb(st, "Lend%d" % i, [128, nch], F32) for i in range(2)]
            ktok = [self.sb(st, "ktok%d" % i, [128, 4, 128], BF16) for i in range(2)]
            PT = [self.sb(st, "PT%d" % i, [128, 4, 128], BF16) for i in range(2)]
            Sbf = [[self.sb(st, "Sbf%d_%d" % (i, h), [128, 256], BF16) for h in range(4)] for i in range(2)]
            if want_out:
                of_sb = [self.sb(st, "of_sb%d" % i, [128, D], F32) for i in range(2)]
                o_sb = self.sb(st, "o_sb", [128, D], F32)
                rs_sb = self.sb(st, "rs_sb", [128, D], F32)
                on_b = self.sb(st, "on_b", [128, D], BF16)
                onT = self.sb(st, "onT", [128, 8, 128], BF16)
                ssh = self.sb(st, "ssh", [128, 4], F32)
                rsh = self.sb(st, "rsh", [128, 4], F32)
                h1 = self.sb(st, "h1", [128, 4], F32)
                h2 = self.sb(st, "h2", [128, 4], F32)
                h3 = self.sb(st, "h3", [128, 4], F32)
            if zero_state:
                for h in range(4):
                    P.op(POOL, lambda h=h: nc.gpsimd.memset(self.S[1][h][:], 0.0), writes=[self.S[1][h]])
            def load_qkl(t):
                t0_ = t * T
                P.dma(SP, qT_sb[:], gs["qT"][:, :, t0_:t0_ + T].rearrange("h p t -> p h t"), writes=[qT_sb])
                P.dma(SP, kT_sb[:], gs["kT"][:, :, t0_:t0_ + T].rearrange("h p t -> p h t"), writes=[kT_sb])
                P.dma(SP, lb[:], gs["lb"][t0_:t0_ + T, :].rearrange("(tt p) c -> p tt c", p=128), writes=[lb])

            load_qkl(nsteps - 1)
            for t in reversed(range(nsteps)):
                t0 = t * T
                P.dma(SP, v_sb[:], gs["v"][t0:t0 + T, :].rearrange("(tt p) c -> p tt c", p=128), writes=[v_sb])
                if want_out:
                    self.prenorm_many(pn, x_src, [(t * nt_c + pos, pos) for pos in range(nt_c)], AT, BT, pT)
                self.gla_prep(st, 1, T, lb, qT_sb, kT_sb, (khT, qhT, edec, Dsb, EK, EQ, Lend))
                if t > 0:
                    load_qkl(t - 1)
                for tt in reversed(range(nt_c)):
                    jt = t * nt_c + tt
                    if want_out:
                        ob = of_sb[tt % 2]
                        P.dma(SP, ob[:], gs["of"][t0 + tt * 128:t0 + (tt + 1) * 128, :], writes=[ob])
                    if want_out:
                        pr = (bank[5], bank[6])
                        for hf in range(2):
                            for k in range(8):
                                P.op(PE, lambda k=k, hf=hf, tt=tt: nc.tensor.matmul(pr[hf][:], lhsT=hT[:, k, tt * 128:(tt + 1) * 128], rhs=w["wr"][:, k, hf * 512:(hf + 1) * 512],
                                                                                           start=(k == 0), stop=(k == 7)), reads=[hT, w["wr"]], writes=[pr[hf]])
                            P.op(ACT, lambda hf=hf: nc.scalar.activation(out=rs_sb[:, hf * 512:(hf + 1) * 512], in_=pr[hf][:], func=AF.Silu), reads=[pr[hf]], writes=[rs_sb])
                        P.op(DVE, lambda: nc.vector.tensor_tensor(out=rs_sb[:], in0=rs_sb[:], in1=w["ng"][:], op=ALU.mult), reads=[rs_sb, w["ng"]], writes=[rs_sb])
                    po = self.gla_scan_tile(1, tt, khT, qhT, edec, v_sb, (ktok, PT, Sbf), want_out)
                    if not want_out:
                        continue
                    for hf in range(2):
                        P.op(DVE, lambda hf=hf, ob=ob: nc.vector.tensor_tensor(out=o_sb[:, hf * 512:(hf + 1) * 512], in0=po[hf][:], in1=ob[:, hf * 512:(hf + 1) * 512], op=ALU.add),
                             reads=[po[hf], ob], writes=[o_sb])
                    for h in range(4):
                        P.op(ACT, lambda h=h: nc.scalar.activation(out=self.junk[:, h * 256:(h + 1) * 256], in_=o_sb[:, h * 256:(h + 1) * 256], func=AF.Square,
                                                                   accum_out=ssh[:, h:h + 1]), reads=[o_sb], writes=[self.junk, ssh])
                    self.rsqrt_pool(rsh, ssh, 4, 1.0 / 256.0)
                    for h in range(4):
                        P.op(DVE, lambda h=h: nc.vector.scalar_tensor_tensor(out=on_b[:, h * 256:(h + 1) * 256], in0=o_sb[:, h * 256:(h + 1) * 256], scalar=rsh[:, h:h + 1],
                                                                             in1=rs_sb[:, h * 256:(h + 1) * 256], op0=ALU.mult, op1=ALU.mult),
                             reads=[o_sb, rsh, rs_sb], writes=[on_b])
                    for k in range(8):
                        P.op(PE, lambda k=k: nc.tensor.transpose(out=pT[:, k * 128:(k + 1) * 128], in_=on_b[:, k * 128:(k + 1) * 128], identity=self.identb[:]),
                             reads=[on_b, self.identb], writes=[pT])
                    P.op(ACT, lambda: nc.scalar.copy(out=onT[:].rearrange("p k t -> p (k t)"), in_=pT[:]), reads=[pT], writes=[onT])
                    py = (bank[5], bank[6])
                    for hf in range(2):
                        for k in range(8):
                            P.op(PE, lambda k=k, hf=hf: nc.tensor.matmul(py[hf][:], lhsT=onT[:, k, :], rhs=w["wo"][:, k, hf * 512:(hf + 1) * 512],
                                                                        start=(k == 0), stop=(k == 7)), reads=[onT, w["wo"]], writes=[py[hf]])
                    self.post_tile(po_, py, x_src, x_dst, jt, G)
            P.flush()

def _consts():
    c = np.zeros((128, 6, 128), np.float32)
    i = np.arange(128)
    c[:, 0, :] = np.eye(128)
    c[:, 1, :] = 1.0 / 1024.0
    same = (i[:, None] // 64) == (i[None, :] // 64)
    c[:, 2, :] = ((i[:, None] <= i[None, :]) & same) / 16.0
    c[:, 3, :] = ((i[:, None] >= i[None, :]) & same) / 16.0
    c[:, 4, :] = ((i[:, None] <= i[None, :]) & same) * 1.0
    c[:, 5, :] = ((i[:, None] >= i[None, :]) & same) * 1.0
    return c


def _fm(v, nk):
    v = np.asarray(v)
    lead = v.shape[:-1]
    return np.ascontiguousarray(np.swapaxes(v.reshape(lead + (nk, 128)), -1, -2))


def prep_inputs(inp):
    f = lambda a: np.ascontiguousarray(np.asarray(a, dtype=np.float32))
    shared = {}
    shared["ada_w"] = f(inp["ada_w"])
    shared["ada_b"] = f(inp["ada_b"])
    for k in ("norm_pre_mix", "norm_post_mix", "norm_pre_ffn", "norm_post_ffn"):
        shared[k] = f(inp[k])
    shared["npre_mixT"] = f(_fm(inp["norm_pre_mix"], 8))
    shared["npre_ffnT"] = f(_fm(inp["norm_pre_ffn"], 8))
    shared["cf_w1"] = f(inp["cf_w1"])
    shared["cf_b1T"] = f(_fm(inp["cf_b1"], 16))
    shared["cf_dwT"] = f(np.transpose(np.asarray(inp["cf_dw"]).reshape(2, KW, 8, 128), (0, 3, 2, 1)))
    shared["cf_dwbT"] = f(_fm(inp["cf_dwb"], 8))
    shared["cf_lngT"] = f(_fm(inp["cf_ln_g"], 8))
    shared["cf_lnbT"] = f(_fm(inp["cf_ln_b"], 8))
    shared["cf_w2"] = f(inp["cf_w2"])
    shared["cf_b2"] = f(inp["cf_b2"])
    shared["gla_wq"] = f(inp["gla_wq"])
    shared["gla_wk"] = f(inp["gla_wk"])
    shared["gla_wv"] = f(inp["gla_wv"])
    shared["gla_wr"] = f(inp["gla_wr"])
    wg1 = np.asarray(inp["gla_wg1"])
    shared["gla_wg1c"] = f(np.concatenate([wg1[:, 0], wg1[:, 1]], axis=-1))
    wg2 = np.asarray(inp["gla_wg2"])
    blk = np.zeros((2, 32, 1024), np.float32)
    blk[:, 0:16, 0:512] = wg2[:, 0]
    blk[:, 16:32, 512:1024] = wg2[:, 1]
    shared["gla_wg2c"] = blk
    bg = np.asarray(inp["gla_bg"])
    shared["gla_bgc"] = f(bg.reshape(2, 1024))
    ng = np.asarray(inp["gla_norm_g"])
    shared["gla_ngc"] = f(np.tile(ng, (1, 4)))
    shared["gla_wo"] = f(inp["gla_wo"])
    shared["ffn_wa"] = f(inp["ffn_wa"])
    shared["ffn_wb"] = f(inp["ffn_wb"])
    shared["ffn_dwT"] = f(np.transpose(np.asarray(inp["ffn_dw"]).reshape(DEPTH, 9, NFC, 128), (0, 3, 2, 1)))
    shared["ffn_dwbT"] = f(_fm(inp["ffn_dwb"], NFC))
    shared["ffn_wo"] = f(inp["ffn_wo"])
    shared["consts"] = _consts()
    x = np.asarray(inp["x"], dtype=np.float32)
    c = np.asarray(inp["c"], dtype=np.float32)
    ctx = np.asarray(inp["ctx"], dtype=np.float32)
    c_ctx = np.asarray(inp["c_ctx"], dtype=np.float32)
    maps = []
    for b in range(x.shape[0]):
        m = dict(shared)
        m["x"] = np.ascontiguousarray(x[b])
        m["ctx"] = np.ascontiguousarray(ctx[b])
        cT = np.stack([c[b].reshape(8, 128).T, c_ctx.reshape(8, 128).T], axis=-1)
        m["cT"] = np.ascontiguousarray(cT)
        maps.append(m)
    return maps


_NC_CACHE = {}


def kernel(**inputs):
    maps = prep_inputs(inputs)
    if "nc" not in _NC_CACHE:
        _NC_CACHE["nc"] = KB().build()
    nc = _NC_CACHE["nc"]
    res = run_bass_kernel_spmd(nc, maps, core_ids=list(range(NCORES)))
    return np.stack([np.asarray(r["out"], dtype=np.float32) for r in res.results], axis=0)
```
